# Optimizing a Trainium2 kernel written in Bass

```python
import math
import jax
import jax.numpy as jnp
from jax import lax
import numpy as np

D_MODEL = 2048
BATCH = 8
SEQ = 4096
DEPTH = 2

CTX_LEN = 256
GRID_W = 64
HEAD_DIM = 128
ROPE_QUARTER = HEAD_DIM // 4
ROPE_THETA = 10000.0
Q_BLOCK = 128
NORM_EPS = 1e-6
D_FF = 5632
N_MOD = 9

HG_HEADS = 8
HG_DK = 128
HG_DV = 128
HG_WIDTH = HG_HEADS * HG_DK
HG_CHUNK = 64
DA_HEADS = 4
DA_DK = 128
DA_DV = 2 * DA_DK
DA_QK_WIDTH = 2 * DA_HEADS * DA_DK
DA_V_WIDTH = DA_HEADS * DA_DV
EVEN_IN = 5 * HG_WIDTH + 2 * DA_QK_WIDTH + DA_V_WIDTH
EVEN_OUT = HG_HEADS * HG_DV + DA_V_WIDTH

GQA_HEADS = 8
GQA_KV_HEADS = 2
GQA_Q_WIDTH = GQA_HEADS * HEAD_DIM
GQA_KV_WIDTH = GQA_KV_HEADS * HEAD_DIM
RET_HEADS = 4
RET_DK = 128
RET_DV = 256
RET_CHUNK = 128
RET_QK_WIDTH = RET_HEADS * RET_DK
RET_V_WIDTH = RET_HEADS * RET_DV
ODD_IN = GQA_Q_WIDTH + 2 * GQA_KV_WIDTH + 2 * RET_QK_WIDTH + 2 * RET_V_WIDTH
ODD_OUT = GQA_Q_WIDTH + RET_V_WIDTH

N_EVEN = (DEPTH + 1) // 2
N_ODD = DEPTH // 2

kernel_name = "hybrid_hgrn2_diffattn_gqa_retention_dit"


def rms_norm(x, g):
    xf = x.astype(jnp.float32)
    y = xf * lax.rsqrt(jnp.mean(xf * xf, axis=-1, keepdims=True) + NORM_EPS)
    return (y * g.astype(jnp.float32)).astype(x.dtype)


def modulate(x, g, shift, scale):
    return rms_norm(x, g) * (1.0 + scale) + shift


def swiglu(h, w13, w2):
    a, b = jnp.split(h @ w13, 2, axis=-1)
    return (jax.nn.silu(a) * b) @ w2


def rope_tables(n_tokens):
    rows = n_tokens // GRID_W
    row = jnp.repeat(jnp.arange(rows), GRID_W)
    col = jnp.tile(jnp.arange(GRID_W), rows)
    pos = jnp.stack([row, col], axis=-1).astype(jnp.float32)
    inv = ROPE_THETA ** (-jnp.arange(ROPE_QUARTER, dtype=jnp.float32) / ROPE_QUARTER)
    ang = pos[:, :, None] * inv
    return jnp.cos(ang)[:, None], jnp.sin(ang)[:, None]


def apply_rope(x, cos, sin):
    xr = x.reshape(*x.shape[:-1], 2, 2, ROPE_QUARTER)
    x1, x2 = xr[..., 0, :], xr[..., 1, :]
    c, s = cos.astype(x.dtype), sin.astype(x.dtype)
    return jnp.stack([x1 * c - x2 * s, x2 * c + x1 * s], axis=-2).reshape(x.shape)


def block_attention(q, k, v):
    b, t, h, dk = q.shape
    nb = t // Q_BLOCK
    qb = jnp.moveaxis(q.reshape(b, nb, Q_BLOCK, h, dk), 1, 0)

    def attend(qi):
        s = jnp.einsum('bqhd,bkhd->bhqk', qi, k).astype(jnp.float32)
        p = jax.nn.softmax(s, axis=-1).astype(v.dtype)
        return jnp.einsum('bhqk,bkhd->bqhd', p, v)

    o = lax.map(attend, qb)
    return jnp.moveaxis(o, 0, 1).reshape(b, t, h, v.shape[-1])


def gla_chunked(q, k, v, log_f, s0):
    b, h, t, dk = q.shape
    n = t // HG_CHUNK

    def chunks(a):
        return jnp.moveaxis(a.reshape(b, h, n, HG_CHUNK, a.shape[-1]), 2, 0)

    lower = jnp.tril(jnp.ones((HG_CHUNK, HG_CHUNK), dtype=bool))[:, :, None]

    def step(state, inp):
        qc, kc, vc, gc = inp
        bcum = jnp.cumsum(gc, axis=2)
        rel = jnp.where(lower, bcum[:, :, :, None, :] - bcum[:, :, None, :, :], -jnp.inf)
        scores = jnp.einsum('bhik,bhjk,bhijk->bhij', qc, kc, jnp.exp(rel))
        out = scores @ vc + jnp.einsum('bhik,bhkv->bhiv', qc * jnp.exp(bcum), state)
        b_last = bcum[:, :, -1:, :]
        state = (jnp.swapaxes(jnp.exp(b_last), -1, -2) * state
                 + jnp.einsum('bhjk,bhjv->bhkv', kc * jnp.exp(b_last - bcum), vc))
        return state, out

    s_fin, o = lax.scan(step, s0, (chunks(q), chunks(k), chunks(v), chunks(log_f)))
    return jnp.moveaxis(o, 0, 2).reshape(b, h, t, v.shape[-1]), s_fin


def retention_chunked(q, k, v, log_gamma, s0):
    b, h, t, dk = q.shape
    n = t // RET_CHUNK
    pos = jnp.arange(RET_CHUNK, dtype=jnp.float32)
    rel = pos[:, None] - pos[None, :]
    lg = log_gamma[:, None, None]
    decay = jnp.where(rel >= 0, jnp.exp(jnp.maximum(rel, 0.0) * lg), 0.0)
    q_dec = jnp.exp((pos + 1.0)[None, :, None] * lg)
    k_dec = jnp.exp((RET_CHUNK - 1.0 - pos)[None, :, None] * lg)
    c_dec = jnp.exp(RET_CHUNK * lg)

    def chunks(a):
        return jnp.moveaxis(a.reshape(b, h, n, RET_CHUNK, a.shape[-1]), 2, 0)

    def step(state, inp):
        qc, kc, vc = inp
        scores = jnp.einsum('bhid,bhjd->bhij', qc, kc) * decay
        out = scores @ vc + jnp.einsum('bhid,bhdv->bhiv', qc * q_dec, state)
        state = c_dec * state + jnp.einsum('bhjd,bhjv->bhdv', kc * k_dec, vc)
        return state, out

    s_fin, o = lax.scan(step, s0, (chunks(q), chunks(k), chunks(v)))
    return jnp.moveaxis(o, 0, 2).reshape(b, h, t, v.shape[-1]), s_fin


def flip_t(a):
    return a[:, :, ::-1]


def two_stream(scan, ctx_args, lat_args, s0):
    o_c, s_c = scan(*ctx_args, s0)
    o_x, _ = scan(*lat_args, s_c)
    return o_x, o_c


def bidir_two_stream(scan_f, scan_b, ctx_f, lat_f, ctx_b, lat_b, s0):
    ox_f, oc_f = two_stream(scan_f, ctx_f, lat_f, s0)
    ox_b, oc_b = two_stream(scan_b, tuple(flip_t(a) for a in ctx_b), tuple(flip_t(a) for a in lat_b), s0)
    return ox_f + flip_t(ox_b), oc_f + flip_t(oc_b)


def to_heads(a, n_heads):
    b, t, _ = a.shape
    return a.reshape(b, t, n_heads, -1).transpose(0, 2, 1, 3).astype(jnp.float32)


def even_mixer(h_x, h_c, w_in, w_out, lower_bound, hg_g, lam_p, da_g, layer_idx, cos, sin, ctx_out):
    dt = h_x.dtype
    lam_init = 0.8 - 0.6 * math.exp(-0.3 * layer_idx)
    lam = jnp.exp(jnp.sum(lam_p[0] * lam_p[1])) - jnp.exp(jnp.sum(lam_p[2] * lam_p[3])) + lam_init
    splits = [int(s) for s in np.cumsum([HG_WIDTH] * 5 + [DA_QK_WIDTH, DA_QK_WIDTH])]
    px = jnp.split(h_x @ w_in, splits, axis=-1)
    pc = jnp.split(h_c @ w_in, splits, axis=-1)

    lb_h = lower_bound.reshape(HG_HEADS, 1, HG_DK)

    def hg_prep(p):
        q = jax.nn.silu(to_heads(p[0], HG_HEADS)) * HG_DK ** -0.5
        val = to_heads(p[3], HG_HEADS)

        def forget(z):
            z = to_heads(z, HG_HEADS)
            return (1.0 - lb_h) * jax.nn.sigmoid(-z), jnp.log(lb_h + (1.0 - lb_h) * jax.nn.sigmoid(z))

        k_f, lf_f = forget(p[1])
        k_b, lf_b = forget(p[2])
        return (q, k_f, val, lf_f), (q, k_b, val, lf_b)

    ctx_f, ctx_b = hg_prep(pc)
    lat_f, lat_b = hg_prep(px)
    s0 = jnp.zeros((h_x.shape[0], HG_HEADS, HG_DK, HG_DV), jnp.float32)
    o_hx, o_hc = bidir_two_stream(gla_chunked, gla_chunked, ctx_f, lat_f, ctx_b, lat_b, s0)

    def hg_out(o, gate):
        b, h, t, d = o.shape
        y = rms_norm(o.transpose(0, 2, 1, 3), hg_g) * jax.nn.sigmoid(gate.astype(jnp.float32)).reshape(b, t, h, d)
        return y.reshape(b, t, h * d).astype(dt)

    def da_prep(p, rotate):
        b, t, _ = p[5].shape
        q = p[5].reshape(b, t, 2 * DA_HEADS, DA_DK)
        k = p[6].reshape(b, t, 2 * DA_HEADS, DA_DK)
        v = p[7].reshape(b, t, DA_HEADS, DA_DV)
        if rotate:
            q, k = apply_rope(q, cos, sin), apply_rope(k, cos, sin)
        return q * DA_DK ** -0.5, k, jnp.repeat(v, 2, axis=2)

    def da_out(o):
        b, t = o.shape[:2]
        o5 = o.reshape(b, t, DA_HEADS, 2, DA_DV)
        d = o5[..., 0, :] - lam * o5[..., 1, :]
        return (rms_norm(d, da_g) * (1.0 - lam_init)).reshape(b, t, DA_V_WIDTH).astype(dt)

    q_x, k_x, v_x = da_prep(px, True)
    q_c, k_c, v_c = da_prep(pc, False)
    o_dx = block_attention(q_x, jnp.concatenate([k_x, k_c], axis=1), jnp.concatenate([v_x, v_c], axis=1))
    y_x = jnp.concatenate([hg_out(o_hx, px[4]), da_out(o_dx)], axis=-1) @ w_out
    if not ctx_out:
        return y_x, None
    o_dc = block_attention(q_c, k_c, v_c)
    y_c = jnp.concatenate([hg_out(o_hc, pc[4]), da_out(o_dc)], axis=-1) @ w_out
    return y_x, y_c


def odd_mixer(h_x, h_c, w_in, w_out, q_g, k_g, decay_logit, ret_g, cos, sin, ctx_out):
    dt = h_x.dtype
    splits = [int(s) for s in np.cumsum([GQA_Q_WIDTH, GQA_KV_WIDTH, GQA_KV_WIDTH,
                                         RET_QK_WIDTH, RET_QK_WIDTH, RET_V_WIDTH])]
    px = jnp.split(h_x @ w_in, splits, axis=-1)
    pc = jnp.split(h_c @ w_in, splits, axis=-1)

    rep = GQA_HEADS // GQA_KV_HEADS

    def gqa_prep(p, rotate):
        b, t, _ = p[0].shape
        q = rms_norm(p[0].reshape(b, t, GQA_HEADS, HEAD_DIM), q_g)
        k = rms_norm(p[1].reshape(b, t, GQA_KV_HEADS, HEAD_DIM), k_g)
        v = p[2].reshape(b, t, GQA_KV_HEADS, HEAD_DIM)
        if rotate:
            q, k = apply_rope(q, cos, sin), apply_rope(k, cos, sin)
        return q * HEAD_DIM ** -0.5, jnp.repeat(k, rep, axis=2), jnp.repeat(v, rep, axis=2)

    q_x, k_x, v_x = gqa_prep(px, True)
    q_c, k_c, v_c = gqa_prep(pc, False)
    b, t = h_x.shape[:2]
    o_gx = block_attention(q_x, jnp.concatenate([k_x, k_c], axis=1),
                           jnp.concatenate([v_x, v_c], axis=1)).reshape(b, t, GQA_Q_WIDTH)

    log_gamma = jax.nn.log_sigmoid(decay_logit.astype(jnp.float32))

    def ret_prep(p, rotate):
        bb, tt, _ = p[3].shape
        q = p[3].reshape(bb, tt, RET_HEADS, RET_DK)
        k = p[4].reshape(bb, tt, RET_HEADS, RET_DK)
        v = p[5].reshape(bb, tt, RET_HEADS, RET_DV)
        if rotate:
            q, k = apply_rope(q, cos, sin), apply_rope(k, cos, sin)
        k = k * RET_DK ** -0.5
        return tuple(a.transpose(0, 2, 1, 3).astype(jnp.float32) for a in (q, k, v))

    def scan_f(q, k, v, s):
        return retention_chunked(q, k, v, log_gamma[0], s)

    def scan_b(q, k, v, s):
        return retention_chunked(q, k, v, log_gamma[1], s)

    ctx_r = ret_prep(pc, False)
    lat_r = ret_prep(px, True)
    s0 = jnp.zeros((b, RET_HEADS, RET_DK, RET_DV), jnp.float32)
    o_rx, o_rc = bidir_two_stream(scan_f, scan_b, ctx_r, lat_r, ctx_r, lat_r, s0)

    def ret_out(o, gate):
        bb, h, tt, d = o.shape
        y = rms_norm(o.transpose(0, 2, 1, 3), ret_g) * jax.nn.silu(gate.astype(jnp.float32)).reshape(bb, tt, h, d)
        return y.reshape(bb, tt, h * d).astype(dt)

    y_x = jnp.concatenate([o_gx.astype(dt), ret_out(o_rx, px[6])], axis=-1) @ w_out
    if not ctx_out:
        return y_x, None
    o_gc = block_attention(q_c, k_c, v_c).reshape(h_c.shape[0], h_c.shape[1], GQA_Q_WIDTH)
    y_c = jnp.concatenate([o_gc.astype(dt), ret_out(o_rc, pc[6])], axis=-1) @ w_out
    return y_x, y_c


def setup_inputs(seed: int = 0) -> dict:
    key = jax.random.key(seed)
    ks = jax.random.split(key, 22)
    f32 = jnp.float32

    def nrm(i, shape, scale):
        return scale * jax.random.normal(ks[i], shape, f32)

    def gain(i, shape):
        return 1.0 + nrm(i, shape, 0.02)

    gamma0 = 1.0 - 2.0 ** (-5.0 - np.arange(RET_HEADS, dtype=np.float32))
    base_logit = jnp.asarray(np.log(gamma0 / (1.0 - gamma0)), f32)
    return {
        "x": nrm(0, (BATCH, SEQ, D_MODEL), 1.0),
        "c": nrm(1, (BATCH, D_MODEL), 1.0),
        "ctx": nrm(2, (BATCH, CTX_LEN, D_MODEL), 1.0),
        "c_ctx": nrm(3, (D_MODEL,), 1.0),
        "mod_w": nrm(4, (DEPTH, D_MODEL, N_MOD * D_MODEL), D_MODEL ** -0.5),
        "mod_b": nrm(5, (DEPTH, N_MOD * D_MODEL), 0.01),
        "norm_g": gain(6, (DEPTH, 3, D_MODEL)),
        "ffn_w13": nrm(7, (DEPTH, 2, D_MODEL, 2 * D_FF), D_MODEL ** -0.5),
        "ffn_w2": nrm(8, (DEPTH, 2, D_FF, D_MODEL), D_FF ** -0.5),
        "even_w_in": nrm(9, (N_EVEN, D_MODEL, EVEN_IN), D_MODEL ** -0.5),
        "even_w_out": nrm(10, (N_EVEN, EVEN_OUT, D_MODEL), EVEN_OUT ** -0.5),
        "hgrn_lb_logits": nrm(11, (DEPTH + 1, HG_WIDTH), 0.1),
        "hgrn_norm_g": gain(12, (N_EVEN, HG_DV)),
        "diff_lambda": nrm(13, (N_EVEN, 4, DA_DK), 0.1),
        "diff_norm_g": gain(14, (N_EVEN, DA_DV)),
        "odd_w_in": nrm(15, (N_ODD, D_MODEL, ODD_IN), D_MODEL ** -0.5),
        "odd_w_out": nrm(16, (N_ODD, ODD_OUT, D_MODEL), ODD_OUT ** -0.5),
        "gqa_q_norm_g": gain(17, (N_ODD, HEAD_DIM)),
        "gqa_k_norm_g": gain(18, (N_ODD, HEAD_DIM)),
        "ret_decay_logit": base_logit + nrm(19, (N_ODD, 2, RET_HEADS), 0.1),
        "ret_norm_g": gain(20, (N_ODD, RET_DV)),
        "final_norm_g": gain(21, (D_MODEL,)),
    }


def reference(x, c, ctx, c_ctx, mod_w, mod_b, norm_g, ffn_w13, ffn_w2, even_w_in, even_w_out,
              hgrn_lb_logits, hgrn_norm_g, diff_lambda, diff_norm_g, odd_w_in, odd_w_out,
              gqa_q_norm_g, gqa_k_norm_g, ret_decay_logit, ret_norm_g, final_norm_g):
    cos, sin = rope_tables(x.shape[1])
    lower_bounds = jnp.cumsum(jax.nn.softmax(hgrn_lb_logits.astype(jnp.float32), axis=0), axis=0)
    for l in range(DEPTH):
        last = l == DEPTH - 1
        m_x = (jax.nn.silu(c) @ mod_w[l] + mod_b[l])[:, None, :]
        m_c = jax.nn.silu(c_ctx) @ mod_w[l] + mod_b[l]
        sx = jnp.split(m_x, N_MOD, axis=-1)
        sc = jnp.split(m_c, N_MOD, axis=-1)

        x = x + 0.5 * sx[2] * swiglu(modulate(x, norm_g[l, 0], sx[0], sx[1]), ffn_w13[l, 0], ffn_w2[l, 0])
        ctx = ctx + 0.5 * sc[2] * swiglu(modulate(ctx, norm_g[l, 0], sc[0], sc[1]), ffn_w13[l, 0], ffn_w2[l, 0])

        h_x = modulate(x, norm_g[l, 1], sx[3], sx[4])
        h_c = modulate(ctx, norm_g[l, 1], sc[3], sc[4])
        if l % 2 == 0:
            e = l // 2
            o_x, o_c = even_mixer(h_x, h_c, even_w_in[e], even_w_out[e], lower_bounds[l], hgrn_norm_g[e],
                                  diff_lambda[e], diff_norm_g[e], l, cos, sin, not last)
        else:
            o = l // 2
            o_x, o_c = odd_mixer(h_x, h_c, odd_w_in[o], odd_w_out[o], gqa_q_norm_g[o], gqa_k_norm_g[o],
                                 ret_decay_logit[o], ret_norm_g[o], cos, sin, not last)
        x = x + sx[5] * o_x

        x = x + 0.5 * sx[8] * swiglu(modulate(x, norm_g[l, 2], sx[6], sx[7]), ffn_w13[l, 1], ffn_w2[l, 1])
        if not last:
            ctx = ctx + sc[5] * o_c
            ctx = ctx + 0.5 * sc[8] * swiglu(modulate(ctx, norm_g[l, 2], sc[6], sc[7]), ffn_w13[l, 1], ffn_w2[l, 1])
    return rms_norm(x, final_norm_g)
```

```python
import contextlib
import math
import numpy as np
import concourse.bass as bass
import concourse.mybir as mybir
from concourse.bass_utils import run_bass_kernel_spmd

F32 = mybir.dt.float32
BF16 = mybir.dt.bfloat16
AF = mybir.ActivationFunctionType
ALU = mybir.AluOpType
AX = mybir.AxisListType

D = 2048
KC = 16
SEQ = 4096
NCTX = 256
T = SEQ + NCTX
DFF = 5632
FC = DFF // 128
EPS = 1e-6
NCH = T // 64
EVEN_IN = 8192
SELF_RAW_DIST = 1000000
ODD_IN = 4608

R_C, R_CCTX, R_NG = 0, 16, 32
R_FG = 128
R_LB = 144
R_HGG = 168
R_DAG = 169
R_QG, R_KG = 171, 172
R_RG = 173
R_MB = 256
R_TOT = 640


class Buf:
    __slots__ = ("t", "w", "r", "dsem", "name")

    def __init__(self, t, name):
        self.t = t
        self.w = None
        self.r = {}
        self.dsem = None
        self.name = name


class Prog:
    CE = ("pe", "act", "dve", "pool")
    ENG = ("pe", "act", "dve", "pool", "sp")

    def __init__(self, nc, stack):
        self.nc = nc
        self.stack = stack
        self.esem = {e: stack.enter_context(nc.semaphore("es_" + e)) for e in self.CE}
        self.cnt = {e: 0 for e in self.CE}
        self.ops = {e: [] for e in self.ENG}
        self.known = {e: {} for e in self.ENG}
        self.dsems = []
        self.dtot = []
        self.dfree = []
        self.phase_bufs = []
        self.uid = 0
        self.nops = 0

    def sb(self, name, shape, dt, st=None):
        self.uid += 1
        t = (st or self.stack).enter_context(self.nc.sbuf_tensor(f"{name}_{self.uid}", list(shape), dt))
        b = Buf(t, name)
        if st is not None:
            self.phase_bufs.append(b)
        return b

    def ps(self, name, shape, dt, st):
        self.uid += 1
        t = st.enter_context(self.nc.psum_tensor(f"{name}_{self.uid}", list(shape), dt))
        return Buf(t, name)

    def _dsem(self, b):
        if b.dsem is None:
            if self.dfree:
                b.dsem = self.dfree.pop()
            else:
                s = self.stack.enter_context(self.nc.semaphore(f"ds{len(self.dsems)}"))
                self.dsems.append(s)
                self.dtot.append(0)
                b.dsem = len(self.dsems) - 1
        return b.dsem

    def _need(self, eng, waits, ev, raw=False):
        if ev is None:
            return
        k, v = ev
        if k == eng and eng != "pool":
            if eng == "pe" or not raw or v <= self.cnt[eng] - SELF_RAW_DIST:
                return
        if self.known[eng].get(k, 0) >= v:
            return
        if waits.get(k, 0) < v:
            waits[k] = v

    def _deps(self, eng, reads, writes):
        waits = {}
        for b in reads:
            self._need(eng, waits, b.w, raw=True)
        for b in writes:
            self._need(eng, waits, b.w)
            for k, v in b.r.items():
                self._need(eng, waits, (k, v))
        for k, v in waits.items():
            self.known[eng][k] = v
        return tuple(waits.items())

    def op(self, eng, fn, reads=(), writes=(), inc=True):
        waits = self._deps(eng, reads, writes)
        seq = self.cnt[eng] + 1
        if inc:
            self.cnt[eng] = seq
        self.ops[eng].append((waits, fn, 1 if inc else 0))
        self.nops += 1
        for b in reads:
            if b not in writes:
                b.r[eng] = seq
        for b in writes:
            b.w = (eng, seq)
            b.r = {}

    def dma(self, q, out_ap, in_ap, sbuf, reads=(), writes=()):
        waits = self._deps(q, reads, writes)
        di = self._dsem(sbuf)
        self.dtot[di] += 16
        key, val = ("d", di), self.dtot[di]
        self.ops[q].append((waits, (out_ap, in_ap), ("dma", di)))
        self.nops += 1
        for b in reads:
            b.r[key] = val
        for b in writes:
            b.w = (key, val)
            b.r = {}

    def barrier(self):
        for e in self.ENG:
            waits = {}
            for k in self.CE:
                if k != e and self.cnt[k] > 0:
                    self._need(e, waits, (k, self.cnt[k]))
            for i, tot in enumerate(self.dtot):
                if tot > 0:
                    self._need(e, waits, (("d", i), tot))
            for k, v in waits.items():
                self.known[e][k] = v
            if waits:
                self.ops[e].append((tuple(waits.items()), None, 0))

    def end_phase(self):
        self.barrier()
        for b in self.phase_bufs:
            if b.dsem is not None:
                self.dfree.append(b.dsem)
                b.dsem = None
        self.phase_bufs = []
        self.flush()

    def _sem(self, k):
        return self.dsems[k[1]] if isinstance(k, tuple) else self.esem[k]

    def _emit(self, name, e):
        for waits, fn, inc in self.ops[name]:
            for k, v in waits:
                e.wait_ge(self._sem(k), v)
            if fn is None:
                continue
            if isinstance(inc, tuple):
                o, i = fn
                e.dma_start(out=o, in_=i).then_inc(self.dsems[inc[1]], 16)
            else:
                ins = fn(e)
                if inc:
                    ins.then_inc(self.esem[name], 1)

    def flush(self):
        self.nflush = getattr(self, "nflush", 0) + 1
        with self.nc.named_scope(f"ph{self.nflush:02d}"), self.nc.Block() as block:
            @block.tensor
            def _(e):
                self._emit("pe", e)

            @block.scalar
            def _(e):
                self._emit("act", e)

            @block.vector
            def _(e):
                self._emit("dve", e)

            @block.gpsimd
            def _(e):
                self._emit("pool", e)

            @block.sync
            def _(e):
                self._emit("sp", e)
        self.ops = {e: [] for e in self.ENG}


def token_tiles(with_ctx=True):
    tl = [(0, NCTX, 1)] if with_ctx else []
    for i in range(SEQ // 512):
        tl.append((NCTX + 512 * i, 512, 0))
    return tl


class Ctx:
    pass


def build_program(debug=None):
    nc = bass.Bass("TRN2", target_bir_lowering=False)
    I = {}

    def din(name, shape, dt=F32):
        I[name] = nc.dram_tensor(name, list(shape), dt, kind="ExternalInput").ap()
        return I[name]

    din("x", [SEQ, D]); din("ctx", [NCTX, D]); din("vecs", [R_TOT, 128])
    din("lamrep", [128, 512]); din("decrep", [128, 8])
    din("mod_w", [2, D, 9 * D]); din("ffn_w13", [2, 2, D, 2 * DFF]); din("ffn_w2", [2, 2, DFF, D])
    din("even_w_in", [D, EVEN_IN]); din("even_w_out", [D, D])
    din("odd_w_in", [D, ODD_IN]); din("odd_w_out", [D, D])
    din("ropec", [128, T]); din("ropes", [128, T]); din("rmat", [128, 128])
    din("masks", [64, 128])
    out = nc.dram_tensor("out", [SEQ, D], F32, kind="ExternalOutput").ap()

    S = {}

    def scratch(name, shape, dt):
        kind = "ExternalOutput" if (debug and name in debug) else "Internal"
        S[name] = nc.dram_tensor("s_" + name, list(shape), dt, kind=kind).ap()
        return S[name]

    scratch("XT", [D, T], F32)
    scratch("W13B", [2, 2, 44, 128, 4096], BF16)
    scratch("W2B", [2, 2, 16, 128, 5632], BF16)
    scratch("WEB", [16, 128, 8192], BF16)
    scratch("WOB", [9, 128, 8192], BF16)
    if debug and "LBD" in debug:
        scratch("LBD", [4, 128, 96], F32)
    if debug and "HD" in debug:
        scratch("HD", [5, 128, 512], F32)
    if debug and "MTD" in debug:
        scratch("MTD", [128, 576 + 192 + 192], F32)
    scratch("YM", [D, T], BF16)
    scratch("GQ", [16, 128, T], BF16); scratch("GK", [16, 128, T], BF16)
    scratch("GKH", [16, T, 128], BF16); scratch("GV", [T, 1024], BF16)
    scratch("GDL", [16, 128, NCH], F32); scratch("GG", [1024, T], F32)
    scratch("OH", [2, 1024, T], F32)
    scratch("AQ", [1024, T], BF16); scratch("AK", [1024, T], BF16); scratch("AV", [T, 1024], BF16)
    scratch("BQ", [1024, T], BF16); scratch("BK", [256, T], BF16); scratch("BV", [T, 256], BF16)

    with contextlib.ExitStack() as stack:
        P = Prog(nc, stack)
        C = Ctx()
        C.nc, C.P, C.I, C.S, C.out = nc, P, I, S, out
        C.stop = debug.get("stop") if debug else None
        setup_phase(C)
        stop = debug.get("stop") if debug else None
        if stop == "setup":
            return nc
        phases = [
            ("pre", lambda: precast_phase(C)),
            ("in", lambda: input_phase(C)),
            ("f00", lambda: ffn_phase(C, 0, 0, token_tiles(True))),
            ("m0", lambda: even_mixer(C)),
            ("f01", lambda: ffn_phase(C, 0, 1, token_tiles(True))),
            ("f10", lambda: ffn_phase(C, 1, 0, token_tiles(True))),
            ("m1", lambda: odd_mixer(C)),
            ("f11", lambda: ffn_phase(C, 1, 1, token_tiles(False))),
            ("fin", lambda: final_phase(C)),
        ]
        def lbdump(ii):
            if "LBD" in S:
                P.dma("sp", S["LBD"][ii][:, 0:64], C.small.t[:], C.small, reads=[C.small])
                P.dma("sp", S["LBD"][ii][:, 64:80], C.lb.t[:], C.lb, reads=[C.lb])
        for pi, (name, fn) in enumerate(phases):
            if pi < 4:
                lbdump(pi)
            fn()
            if stop is not None and stop.startswith(name):
                break
    return nc


def setup_phase(C):
    nc, P, I = C.nc, C.P, C.I
    g = P.stack
    C.ident = P.sb("ident", [128, 128], F32)
    C.identb = P.sb("identb", [128, 128], BF16)
    C.ones = P.sb("ones", [128, 128], F32)
    C.onesb = P.sb("onesb", [128, 128], BF16)
    C.ones512 = P.sb("ones512", [128, 512], F32)
    C.VT = P.sb("VT", [128, R_TOT], F32)
    C.MT = P.sb("MT", [128, 2, 144, 2], F32)
    C.GS = P.sb("GS", [128, 2, 3, 2, 16], F32)
    C.HG = P.sb("HG", [128, 2, 3, 2, 16], F32)
    C.rmat = P.sb("rmatf", [128, 128], F32)
    C.masks = P.sb("masks", [64, 128], F32)
    C.small = P.sb("small", [128, 64], F32)
    C.lb = P.sb("lb", [128, 16], F32)
    C.epsb = P.sb("epsb", [128, 1], F32)
    C.rmask = P.sb("rmask", [128, 512], F32)

    with contextlib.ExitStack() as ph:
        P.op("pool", lambda e: e.iota(C.ident.t[:], pattern=[[1, 128]], base=0, channel_multiplier=-1,
                                      allow_small_or_imprecise_dtypes=True), writes=[C.ident])
        P.op("pool", lambda e: e.tensor_single_scalar(out=C.ident.t[:], in_=C.ident.t[:], scalar=0.0,
                                                      op=ALU.is_equal), reads=[C.ident], writes=[C.ident])
        P.op("dve", lambda e: e.tensor_copy(out=C.identb.t[:], in_=C.ident.t[:]), reads=[C.ident], writes=[C.identb])
        P.op("dve", lambda e: e.memset(C.ones.t[:], 1.0), writes=[C.ones])
        P.op("dve", lambda e: e.memset(C.onesb.t[:], 1.0), writes=[C.onesb])
        P.op("dve", lambda e: e.memset(C.ones512.t[:], 1.0), writes=[C.ones512])
        P.op("dve", lambda e: e.memset(C.epsb.t[:], EPS), writes=[C.epsb])
        P.op("dve", lambda e: e.memset(C.rmask.t[:], 1.0), writes=[C.rmask])
        P.op("dve", lambda e: e.memset(C.rmask.t[:].rearrange("p (c i) -> p c i", i=64)[:, :, 0:1], 0.0), writes=[C.rmask])
        P.dma("sp", C.rmat.t[:], I["rmat"], C.rmat, writes=[C.rmat])
        P.dma("sp", C.masks.t[:], I["masks"], C.masks, writes=[C.masks])

        vrow = P.sb("vrow", [128, 5, 128], F32, ph)
        P.dma("sp", vrow.t[:], I["vecs"].rearrange("(a p) f -> p a f", p=128), vrow, writes=[vrow])
        pst = P.ps("pst", [128, 512], F32, ph)
        for a in range(5):
            P.op("pe", lambda e, a=a: e.transpose(pst.t[:, 0:128], vrow.t[:, a, :], C.ident.t[:]),
                 reads=[vrow, C.ident], writes=[pst])
            P.op("dve", lambda e, a=a: e.tensor_copy(out=C.VT.t[:, a * 128:(a + 1) * 128], in_=pst.t[:, 0:128]),
                 reads=[pst], writes=[C.VT])

        scT = P.sb("scT", [128, 16, 2], F32, ph)
        P.op("act", lambda e: e.activation(out=scT.t[:, :, 0], in_=C.VT.t[:, R_C:R_C + 16], func=AF.Silu),
             reads=[C.VT], writes=[scT])
        P.op("act", lambda e: e.activation(out=scT.t[:, :, 1], in_=C.VT.t[:, R_CCTX:R_CCTX + 16], func=AF.Silu),
             reads=[C.VT], writes=[scT])

        wm = [P.sb(f"wm{j}", [128, 16, 512], F32, ph) for j in range(2)]
        psm = [P.ps(f"psm{j}", [128, 512], F32, ph) for j in range(2)]
        pstm = [P.ps(f"pstm{j}", [128, 8], F32, ph) for j in range(2)]
        mrow = [P.sb(f"mrow{j}", [2, 512], F32, ph) for j in range(2)]
        it = 0
        for l in range(2):
            wv = I["mod_w"][l].rearrange("(kc p) n -> p kc n", p=128)
            for ng in range(36):
                wb, pm, mr, pT = wm[it % 2], psm[it % 2], mrow[it % 2], pstm[it % 2]
                P.dma("sp", wb.t[:], wv[:, :, ng * 512:(ng + 1) * 512], wb, writes=[wb])
                for kc in range(16):
                    P.op("pe", lambda e, wb=wb, pm=pm, kc=kc: e.matmul(
                        pm.t[0:2, :], lhsT=scT.t[:, kc, :], rhs=wb.t[:, kc, :],
                        start=(kc == 0), stop=(kc == 15)), reads=[wb, scT], writes=[pm], inc=(kc == 15))
                P.op("act", lambda e, pm=pm, mr=mr: e.copy(out=mr.t[:], in_=pm.t[0:2, :]), reads=[pm], writes=[mr])
                for j in range(4):
                    P.op("pe", lambda e, mr=mr, pT=pT, j=j: e.transpose(
                        pT.t[:, j * 2:(j + 1) * 2], mr.t[0:2, j * 128:(j + 1) * 128], C.ident.t[0:2, 0:2]),
                        reads=[mr, C.ident], writes=[pT], inc=(j == 3))
                for j in range(4):
                    ch = ng * 4 + j
                    P.op("dve", lambda e, pT=pT, l=l, ch=ch, j=j: e.tensor_scalar(
                        out=C.MT.t[:, l, ch, :], in0=pT.t[:, j * 2:(j + 1) * 2],
                        scalar1=C.VT.t[:, R_MB + l * 144 + ch:R_MB + l * 144 + ch + 1],
                        scalar2=None, op0=ALU.add), reads=[pT, C.VT], writes=[C.MT])
                it += 1
        for l in range(2):
            for i in range(3):
                for s in range(2):
                    c0 = 16 * (3 * i)
                    P.op("dve", lambda e, l=l, i=i, s=s, c0=c0: e.scalar_tensor_tensor(
                        out=C.GS.t[:, l, i, s, :], in0=C.MT.t[:, l, c0 + 16:c0 + 32, s], scalar=1.0,
                        in1=C.VT.t[:, R_NG + (l * 3 + i) * 16:R_NG + (l * 3 + i) * 16 + 16],
                        op0=ALU.add, op1=ALU.mult), reads=[C.MT, C.VT], writes=[C.GS])
                    P.op("dve", lambda e, l=l, i=i, s=s, c0=c0: e.tensor_scalar(
                        out=C.HG.t[:, l, i, s, :], in0=C.MT.t[:, l, c0 + 32:c0 + 48, s],
                        scalar1=(1.0 if i == 1 else 0.5), scalar2=None, op0=ALU.mult), reads=[C.MT], writes=[C.HG])

        sm = C.small
        ex = P.sb("lbex", [128, 24], F32, ph)
        P.op("act", lambda e: e.activation(out=ex.t[:], in_=C.VT.t[:, R_LB:R_LB + 24], func=AF.Exp),
             reads=[C.VT], writes=[ex])
        den = P.sb("lbden", [128, 8], F32, ph)
        P.op("dve", lambda e: e.tensor_tensor(out=den.t[:], in0=ex.t[:, 0:8], in1=ex.t[:, 8:16], op=ALU.add),
             reads=[ex], writes=[den])
        P.op("dve", lambda e: e.tensor_tensor(out=den.t[:], in0=den.t[:], in1=ex.t[:, 16:24], op=ALU.add),
             reads=[ex, den], writes=[den])
        P.op("dve", lambda e: e.reciprocal(out=den.t[:], in_=den.t[:]), reads=[den], writes=[den])
        P.op("dve", lambda e: e.tensor_tensor(out=C.lb.t[:, 0:8], in0=ex.t[:, 0:8], in1=den.t[:], op=ALU.mult),
             reads=[ex, den], writes=[C.lb])
        P.op("dve", lambda e: e.tensor_scalar(out=C.lb.t[:, 8:16], in0=C.lb.t[:, 0:8], scalar1=-1.0, scalar2=1.0,
                                              op0=ALU.mult, op1=ALU.add), reads=[C.lb], writes=[C.lb])

        lr = P.sb("lamrep", [128, 512], F32, ph)
        P.dma("sp", lr.t[:], I["lamrep"], lr, writes=[lr])
        pr = P.sb("lampr", [128, 256], F32, ph)
        P.op("dve", lambda e: e.tensor_tensor(out=pr.t[:, 0:128], in0=lr.t[:, 0:128], in1=lr.t[:, 128:256], op=ALU.mult),
             reads=[lr], writes=[pr])
        P.op("dve", lambda e: e.tensor_tensor(out=pr.t[:, 128:256], in0=lr.t[:, 256:384], in1=lr.t[:, 384:512], op=ALU.mult),
             reads=[lr], writes=[pr])
        P.op("dve", lambda e: e.reduce_sum(out=sm.t[:, 1:2], in_=pr.t[:, 0:128], axis=AX.X), reads=[pr], writes=[sm])
        P.op("dve", lambda e: e.reduce_sum(out=sm.t[:, 2:3], in_=pr.t[:, 128:256], axis=AX.X), reads=[pr], writes=[sm])
        P.op("act", lambda e: e.activation(out=sm.t[:, 1:3], in_=sm.t[:, 1:3], func=AF.Exp), reads=[sm], writes=[sm])
        lam_init = 0.8 - 0.6 * math.exp(-0.3 * 0)
        P.op("dve", lambda e: e.tensor_tensor(out=sm.t[:, 0:1], in0=sm.t[:, 2:3], in1=sm.t[:, 1:2], op=ALU.subtract),
             reads=[sm], writes=[sm])
        P.op("dve", lambda e: e.tensor_scalar(out=sm.t[:, 0:1], in0=sm.t[:, 0:1], scalar1=-lam_init, scalar2=None,
                                              op0=ALU.add), reads=[sm], writes=[sm])
        C.lam_init = lam_init
        if "MTD" in C.S:
            P.dma("sp", C.S["MTD"][:, 0:576], C.MT.t[:].rearrange("p l c s -> p (l c s)"), C.MT, reads=[C.MT])
            P.dma("sp", C.S["MTD"][:, 576:768], C.GS.t[:].rearrange("p l i s c -> p (l i s c)"), C.GS, reads=[C.GS])
            P.dma("sp", C.S["MTD"][:, 768:960], C.HG.t[:].rearrange("p l i s c -> p (l i s c)"), C.HG, reads=[C.HG])

        dr = P.sb("decrep", [128, 8], F32, ph)
        P.dma("sp", dr.t[:], I["decrep"], dr, writes=[dr])
        P.op("act", lambda e: e.activation(out=dr.t[:], in_=dr.t[:], func=AF.Exp, scale=-1.0), reads=[dr], writes=[dr])
        P.op("act", lambda e: e.activation(out=sm.t[:, 16:24], in_=dr.t[:], func=AF.Ln, bias=C.ones.t[:, 0:1], scale=1.0),
             reads=[dr, C.ones], writes=[sm])
        P.op("dve", lambda e: e.tensor_scalar(out=sm.t[:, 8:16], in0=sm.t[:, 16:24], scalar1=-1.0, scalar2=None,
                                              op0=ALU.mult), reads=[sm], writes=[sm])
        P.end_phase()


def precast_phase(C):
    P, I, S = C.P, C.I, C.S
    with contextlib.ExitStack() as ph:
        NB = 3
        stg = [P.sb(f"pstg{j}", [128, 8192], F32, ph) for j in range(NB)]
        outb = [P.sb(f"pout{j}", [128, 8192], BF16, ph) for j in range(NB)]
        st = {"n": 0}

        def job(loads, nel, dst):
            j = st["n"] % NB
            eng = "dve" if st["n"] % 2 == 0 else "act"
            st["n"] += 1
            sb_, ob_ = stg[j], outb[j]
            for fn, ap in loads:
                P.dma("sp", fn(sb_.t), ap, sb_, writes=[sb_])
            if eng == "dve":
                P.op("dve", lambda e: e.tensor_copy(out=ob_.t[:, 0:nel], in_=sb_.t[:, 0:nel]), reads=[sb_], writes=[ob_])
            else:
                P.op("act", lambda e: e.copy(out=ob_.t[:, 0:nel], in_=sb_.t[:, 0:nel]), reads=[sb_], writes=[ob_])
            P.dma("pool", dst, ob_.t[:, 0:nel], ob_, reads=[ob_])

        def run_ffn(l, i):
            w13v = I["ffn_w13"][l, i].rearrange("(kc p) n -> p kc n", p=128)
            w2v = I["ffn_w2"][l, i].rearrange("(fc p) d -> p fc d", p=128)
            for g in range(44):
                v4 = lambda t: t[:, 0:4096].rearrange("p (k t n) -> p k t n", k=16, t=2)
                job([(lambda t: v4(t)[:, :, 0, :], w13v[:, :, g * 128:(g + 1) * 128]),
                     (lambda t: v4(t)[:, :, 1, :], w13v[:, :, DFF + g * 128:DFF + (g + 1) * 128])],
                    4096, S["W13B"][l, i, g])
            for g in range(16):
                for hf in range(2):
                    job([(lambda t: t[:, 0:2816].rearrange("p (f n) -> p f n", f=22),
                          w2v[:, hf * 22:(hf + 1) * 22, g * 128:(g + 1) * 128])],
                        2816, S["W2B"][l, i, g][:, hf * 2816:(hf + 1) * 2816])

        def run_win(wname, sname, ng):
            wv = I[wname].rearrange("(kc p) n -> p kc n", p=128)
            for g in range(ng):
                job([(lambda t: t[:].rearrange("p (k n) -> p k n", k=16), wv[:, :, g * 512:(g + 1) * 512])],
                    8192, S[sname][g])

        run_ffn(0, 0)
        run_win("even_w_in", "WEB", 16)
        run_ffn(0, 1)
        run_ffn(1, 0)
        run_win("odd_w_in", "WOB", 9)
        run_ffn(1, 1)
        P.end_phase()


def input_phase(C):
    P, I, S = C.P, C.I, C.S
    with contextlib.ExitStack() as ph:
        xin = [P.sb(f"xin{j}", [128, 4, D], F32, ph) for j in range(2)]
        xo = [P.sb(f"xo{j}", [128, KC, 512], F32, ph) for j in range(2)]
        pt = [P.ps(f"pt{j}", [128, 512], F32, ph) for j in range(4)]
        XTv = S["XT"].rearrange("(c p) t -> p c t", p=128)
        n = 0
        for ti, (t0, tn, s) in enumerate(token_tiles(True)):
            xb, ob = xin[ti % 2], xo[ti % 2]
            nb = tn // 128
            src = I["ctx"] if s == 1 else I["x"][t0 - NCTX:t0 - NCTX + tn, :]
            P.dma("sp", xb.t[:, 0:nb, :], src.rearrange("(a p) d -> p a d", p=128), xb, writes=[xb])
            for c in range(KC):
                pb = pt[n % 4]
                n += 1
                for a in range(nb):
                    P.op("pe", lambda e, pb=pb, xb=xb, a=a, c=c: e.transpose(
                        pb.t[:, a * 128:(a + 1) * 128], xb.t[:, a, c * 128:(c + 1) * 128], C.ident.t[:]),
                        reads=[xb, C.ident], writes=[pb], inc=(a == nb - 1))
                eng = "dve" if c % 2 == 0 else "act"
                if eng == "dve":
                    P.op("dve", lambda e, pb=pb, ob=ob, c=c, tn=tn: e.tensor_copy(out=ob.t[:, c, 0:tn], in_=pb.t[:, 0:tn]),
                         reads=[pb], writes=[ob])
                else:
                    P.op("act", lambda e, pb=pb, ob=ob, c=c, tn=tn: e.copy(out=ob.t[:, c, 0:tn], in_=pb.t[:, 0:tn]),
                         reads=[pb], writes=[ob])
            P.dma("pool", XTv[:, :, t0:t0 + tn], ob.t[:, :, 0:tn], ob, reads=[ob])
        P.end_phase()


def modnorm(C, ph_bufs, xT, hT, tn, l, i, s):
    P = C.P
    sq, pss, rstd, tmp = ph_bufs
    P.op("act", lambda e: e.activation(out=rstd.t[:, 0:tn], in_=xT.t[:, 0, 0:tn], func=AF.Square), reads=[xT], writes=[rstd])
    for kc in range(1, KC):
        sb_ = sq[kc % 2]
        P.op("act", lambda e, sb_=sb_, kc=kc: e.activation(out=sb_.t[:, 0:tn], in_=xT.t[:, kc, 0:tn], func=AF.Square),
             reads=[xT], writes=[sb_])
        P.op("dve", lambda e, sb_=sb_: e.tensor_tensor(out=rstd.t[:, 0:tn], in0=rstd.t[:, 0:tn], in1=sb_.t[:, 0:tn], op=ALU.add),
             reads=[rstd, sb_], writes=[rstd])
    P.op("pe", lambda e: e.matmul(pss.t[:, 0:tn], lhsT=C.ones.t[:], rhs=rstd.t[:, 0:tn], start=True, stop=True),
         reads=[rstd, C.ones], writes=[pss])
    P.op("act", lambda e: e.activation(out=rstd.t[:, 0:tn], in_=pss.t[:, 0:tn], func=AF.Ln, bias=C.epsb.t[:, 0:1], scale=1.0 / D),
         reads=[pss, C.epsb], writes=[rstd])
    P.op("act", lambda e: e.activation(out=rstd.t[:, 0:tn], in_=rstd.t[:, 0:tn], func=AF.Exp, scale=-0.5),
         reads=[rstd], writes=[rstd])
    c0 = 16 * (3 * i)
    for kc in range(KC):
        tb = tmp[kc % 2]
        P.op("dve", lambda e, tb=tb, kc=kc: e.scalar_tensor_tensor(
            out=tb.t[:, 0:tn], in0=xT.t[:, kc, 0:tn], scalar=C.GS.t[:, l, i, s, kc:kc + 1], in1=rstd.t[:, 0:tn],
            op0=ALU.mult, op1=ALU.mult), reads=[xT, C.GS, rstd], writes=[tb])
        P.op("act", lambda e, tb=tb, kc=kc: e.activation(
            out=hT.t[:, kc, 0:tn], in_=tb.t[:, 0:tn], func=AF.Identity, bias=C.MT.t[:, l, c0 + kc, s:s + 1], scale=1.0),
            reads=[tb, C.MT], writes=[hT])


def ffn_phase(C, l, i_ffn, tiles):
    P, I, S = C.P, C.I, C.S
    isub = 0 if i_ffn == 0 else 2
    with contextlib.ExitStack() as ph:
        xTs = [P.sb(f"xT{j}", [128, KC, 512], F32, ph) for j in range(2)]
        hTs = [P.sb(f"hT{j}", [128, KC, 512], BF16, ph) for j in range(1)]
        gT = P.sb("gT", [128, FC, 512], BF16, ph)
        w13b = [P.sb(f"w13b{j}", [128, KC, 2, 128], BF16, ph) for j in range(3)]
        w2b = [P.sb(f"w2b{j}", [128, FC, 128], BF16, ph) for j in range(3)]
        sq = [P.sb(f"sq{j}", [128, 512], F32, ph) for j in range(2)]
        tmp = [P.sb(f"tmp{j}", [128, 512], F32, ph) for j in range(2)]
        rstd = P.sb("rstd", [128, 512], F32, ph)
        sa = [P.sb(f"sa{j}", [128, 512], F32, ph) for j in range(1)]
        pss = P.ps("pss", [128, 512], F32, ph)
        psa = [P.ps(f"psa{j}", [128, 512], F32, ph) for j in range(2)]
        psb = [P.ps(f"psb{j}", [128, 512], F32, ph) for j in range(2)]
        psy = [P.ps(f"psy{j}", [128, 512], F32, ph) for j in range(2)]
        XTv = S["XT"].rearrange("(c p) t -> p c t", p=128)
        cnt = {"gi": 0, "g2": 0}

        def prep(ti, t0, tn, s):
            xT, hT = xTs[ti % 2], hTs[0]
            P.dma("pool", xT.t[:, :, 0:tn], XTv[:, :, t0:t0 + tn], xT, writes=[xT])
            modnorm(C, (sq, pss, rstd, tmp), xT, hT, tn, l, isub, s)

        def tile_body(ti, t0, tn, s, mid):
            xT, hT = xTs[ti % 2], hTs[0]
            for fc in range(FC):
                wb = w13b[cnt["gi"] % 3]
                cnt["gi"] += 1
                P.dma("sp", wb.t[:], S["W13B"][l, i_ffn, fc].rearrange("p (k t n) -> p k t n", k=16, t=2), wb, writes=[wb])
                pa, pb = psa[fc % 2], psb[fc % 2]
                for kc in range(KC):
                    P.op("pe", lambda e, wb=wb, pa=pa, kc=kc: e.matmul(
                        pa.t[:, 0:tn], lhsT=wb.t[:, kc, 0, :], rhs=hT.t[:, kc, 0:tn],
                        start=(kc == 0), stop=(kc == KC - 1)), reads=[wb, hT], writes=[pa], inc=(kc == KC - 1))
                for kc in range(KC):
                    P.op("pe", lambda e, wb=wb, pb=pb, kc=kc: e.matmul(
                        pb.t[:, 0:tn], lhsT=wb.t[:, kc, 1, :], rhs=hT.t[:, kc, 0:tn],
                        start=(kc == 0), stop=(kc == KC - 1)), reads=[wb, hT], writes=[pb], inc=(kc == KC - 1))
                sb_ = sa[0]
                P.op("act", lambda e, sb_=sb_, pa=pa: e.activation(out=sb_.t[:, 0:tn], in_=pa.t[:, 0:tn], func=AF.Silu),
                     reads=[pa], writes=[sb_])
                P.op("dve", lambda e, sb_=sb_, pb=pb, fc=fc: e.tensor_tensor(
                    out=gT.t[:, fc, 0:tn], in0=sb_.t[:, 0:tn], in1=pb.t[:, 0:tn], op=ALU.mult),
                    reads=[sb_, pb], writes=[gT])
            mid()
            for dc in range(KC):
                wb = w2b[cnt["g2"] % 3]
                cnt["g2"] += 1
                P.dma("sp", wb.t[:], S["W2B"][l, i_ffn, dc].rearrange("p (f n) -> p f n", f=FC), wb, writes=[wb])
                py = psy[dc % 2]
                for fc in range(FC):
                    P.op("pe", lambda e, wb=wb, py=py, fc=fc: e.matmul(
                        py.t[:, 0:tn], lhsT=wb.t[:, fc, :], rhs=gT.t[:, fc, 0:tn],
                        start=(fc == 0), stop=(fc == FC - 1)), reads=[wb, gT], writes=[py], inc=(fc == FC - 1))
                P.op("dve", lambda e, py=py, dc=dc: e.scalar_tensor_tensor(
                    out=xT.t[:, dc, 0:tn], in0=py.t[:, 0:tn], scalar=C.HG.t[:, l, isub, s, dc:dc + 1],
                    in1=xT.t[:, dc, 0:tn], op0=ALU.mult, op1=ALU.add), reads=[py, C.HG, xT], writes=[xT])
            P.dma("pool", XTv[:, :, t0:t0 + tn], xT.t[:, :, 0:tn], xT, reads=[xT])

        prep(0, *tiles[0])
        for ti, (t0, tn, s) in enumerate(tiles):
            if ti + 1 < len(tiles):
                mid = lambda ti=ti: prep(ti + 1, *tiles[ti + 1])
            else:
                mid = lambda: None
            tile_body(ti, t0, tn, s, mid)
        P.end_phase()


def final_phase(C):
    P, I, S = C.P, C.I, C.S
    with contextlib.ExitStack() as ph:
        xT = [P.sb(f"fxT{j}", [128, KC, 512], F32, ph) for j in range(2)]
        ob = [P.sb(f"fob{j}", [128, 4, D], F32, ph) for j in range(2)]
        sq = [P.sb(f"fsq{j}", [128, 512], F32, ph) for j in range(2)]
        rstd = P.sb("frstd", [128, 512], F32, ph)
        pss = P.ps("fpss", [128, 512], F32, ph)
        pt = [P.ps(f"fpt{j}", [128, 512], F32, ph) for j in range(4)]
        XTv = S["XT"].rearrange("(c p) t -> p c t", p=128)
        n = 0
        for ti, (t0, tn, s) in enumerate(token_tiles(False)):
            xb, o = xT[ti % 2], ob[ti % 2]
            P.dma("sp", xb.t[:], XTv[:, :, t0:t0 + tn], xb, writes=[xb])
            for kc in range(KC):
                sb_ = sq[kc % 2]
                P.op("act", lambda e, sb_=sb_, kc=kc, xb=xb: e.activation(out=sb_.t[:], in_=xb.t[:, kc, :], func=AF.Square),
                     reads=[xb], writes=[sb_])
                P.op("pe", lambda e, sb_=sb_, kc=kc: e.matmul(pss.t[:], lhsT=C.ones.t[:], rhs=sb_.t[:],
                                                             start=(kc == 0), stop=(kc == KC - 1)),
                     reads=[sb_, C.ones], writes=[pss])
            P.op("act", lambda e: e.activation(out=rstd.t[:], in_=pss.t[:], func=AF.Ln, bias=C.epsb.t[:, 0:1], scale=1.0 / D),
                 reads=[pss, C.epsb], writes=[rstd])
            P.op("act", lambda e: e.activation(out=rstd.t[:], in_=rstd.t[:], func=AF.Exp, scale=-0.5),
                 reads=[rstd], writes=[rstd])
            for kc in range(KC):
                P.op("dve", lambda e, kc=kc, xb=xb: e.scalar_tensor_tensor(
                    out=xb.t[:, kc, :], in0=xb.t[:, kc, :], scalar=C.VT.t[:, R_FG + kc:R_FG + kc + 1], in1=rstd.t[:],
                    op0=ALU.mult, op1=ALU.mult), reads=[xb, C.VT, rstd], writes=[xb])
            for a in range(4):
                for cg in range(4):
                    pb = pt[n % 4]
                    n += 1
                    for cc in range(4):
                        c = cg * 4 + cc
                        P.op("pe", lambda e, pb=pb, xb=xb, a=a, c=c, cc=cc: e.transpose(
                            pb.t[:, cc * 128:(cc + 1) * 128], xb.t[:, c, a * 128:(a + 1) * 128], C.ident.t[:]),
                            reads=[xb, C.ident], writes=[pb], inc=(cc == 3))
                    if n % 2 == 0:
                        P.op("dve", lambda e, pb=pb, o=o, a=a, cg=cg: e.tensor_copy(
                            out=o.t[:, a, cg * 512:(cg + 1) * 512], in_=pb.t[:]), reads=[pb], writes=[o])
                    else:
                        P.op("act", lambda e, pb=pb, o=o, a=a, cg=cg: e.copy(
                            out=o.t[:, a, cg * 512:(cg + 1) * 512], in_=pb.t[:]), reads=[pb], writes=[o])
            r0 = t0 - NCTX
            P.dma("pool", C.out[r0:r0 + tn, :].rearrange("(a p) d -> p a d", p=128), o.t[:], o, reads=[o])
        P.end_phase()


def proj_chunks(C, ph, hT, tn, wv, chunk_list, wbufs, psp, cb, state):
    P = C.P
    groups = {}
    for j in chunk_list:
        groups.setdefault(j // 4, []).append(j)
    for gidx in sorted(groups):
        wb = wbufs[state["w"] % 2]
        state["w"] += 1
        P.dma("pool", wb.t[:], wv[gidx].rearrange("p (k n) -> p k n", k=16), wb, writes=[wb])
        for j in groups[gidx]:
            u = j % 4
            pb = psp[state["p"] % 2]
            state["p"] += 1
            for kc in range(KC):
                P.op("pe", lambda e, wb=wb, pb=pb, kc=kc, u=u: e.matmul(
                    pb.t[:, 0:tn], lhsT=wb.t[:, kc, u * 128:(u + 1) * 128], rhs=hT.t[:, kc, 0:tn],
                    start=(kc == 0), stop=(kc == KC - 1)), reads=[wb, hT], writes=[pb], inc=(kc == KC - 1))
            cb(j, pb)


class Rot:
    def __init__(self, bufs):
        self.b = bufs
        self.i = 0

    def get(self):
        b = self.b[self.i % len(self.b)]
        self.i += 1
        return b


def rope_op(C, R, src, tn, scale, dst_ap, dst_buf, Ct, St):
    P = C.P
    pr = R["psr"].get()
    P.op("pe", lambda e: e.matmul(pr.t[:, 0:tn], lhsT=C.rmat.t[:], rhs=src.t[:, 0:tn], start=True, stop=True),
         reads=[C.rmat, src], writes=[pr])
    t1 = R["f32"].get()
    t2 = R["f32"].get()
    P.op("dve", lambda e: e.scalar_tensor_tensor(out=t1.t[:, 0:tn], in0=src.t[:, 0:tn], scalar=float(scale),
                                                 in1=Ct.t[:, 0:tn], op0=ALU.mult, op1=ALU.mult),
         reads=[src, Ct], writes=[t1])
    P.op("dve", lambda e: e.scalar_tensor_tensor(out=t2.t[:, 0:tn], in0=pr.t[:, 0:tn], scalar=float(scale),
                                                 in1=St.t[:, 0:tn], op0=ALU.mult, op1=ALU.mult),
         reads=[pr, St], writes=[t2])
    P.op("dve", lambda e: e.tensor_tensor(out=dst_ap, in0=t1.t[:, 0:tn], in1=t2.t[:, 0:tn], op=ALU.add),
         reads=[t1, t2], writes=[dst_buf])


def store_tokmajor(C, R, src, tn, dram_rows):
    P = C.P
    nb = tn // 128
    pt = R["pstb"].get()
    for a in range(nb):
        P.op("pe", lambda e, a=a: e.transpose(pt.t[:, a * 128:(a + 1) * 128], src.t[:, a * 128:(a + 1) * 128], C.identb.t[:]),
             reads=[src, C.identb], writes=[pt], inc=(a == nb - 1))
    st = R["tokb"].get()
    P.op("act", lambda e: e.copy(out=st.t[:, 0:nb * 128], in_=pt.t[:, 0:nb * 128]), reads=[pt], writes=[st])
    P.dma("sp", dram_rows.rearrange("(a p) f -> p a f", p=128),
          st.t[:, 0:nb * 128].rearrange("p (a f) -> p a f", f=128), st, reads=[st])


def mk_rot(C, ph, tagp=""):
    P = C.P
    R = {}
    R["f32"] = Rot([P.sb(f"rf{j}", [128, 512], F32, ph) for j in range(6)])
    R["b16"] = Rot([P.sb(f"rb{j}", [128, 512], BF16, ph) for j in range(4)])
    R["tokb"] = Rot([P.sb(f"tk{j}", [128, 512], BF16, ph) for j in range(2)])
    R["psr"] = Rot([P.ps(f"psr{j}", [128, 512], F32, ph) for j in range(1)])
    R["pstb"] = Rot([P.ps(f"pstb{j}", [128, 512], BF16, ph) for j in range(2)])
    return R


def hgrn_gate(C, R, G, pb, tn, dirn, h, QH, t0):
    P, S = C.P, C.S
    nch = tn // 64
    dh = dirn * 8 + h
    A, An, Gl, Tm, BC, E = G["A"], G["F"], G["G"], G["K"], G["BC"], G["E"]
    oml = C.lb.t[:, 8 + h:9 + h]
    lbh = C.lb.t[:, h:h + 1]
    v3 = lambda b_: b_.t[:, 0:tn].rearrange("p (c i) -> p c i", i=64)
    P.op("act", lambda e: e.activation(out=A.t[:, 0:tn], in_=pb.t[:, 0:tn], func=AF.Sigmoid), reads=[pb], writes=[A])
    P.op("act", lambda e: e.activation(out=An.t[:, 0:tn], in_=pb.t[:, 0:tn], func=AF.Sigmoid, scale=-1.0), reads=[pb], writes=[An])
    P.op("act", lambda e: e.activation(out=Gl.t[:, 0:tn], in_=A.t[:, 0:tn], func=AF.Ln, scale=oml, bias=lbh),
         reads=[A, C.lb], writes=[Gl])
    yield
    P.op("dve", lambda e: e.tensor_tensor_scan(out=BC.t[:, 0:tn], data0=C.rmask.t[:, 0:tn], data1=Gl.t[:, 0:tn],
                                               initial=0.0, op0=ALU.mult, op1=ALU.add), reads=[Gl, C.rmask], writes=[BC])
    if dirn == 0:
        Eb = BC
    else:
        Eb = E
        P.op("dve", lambda e: e.tensor_tensor(out=E.t[:, 0:tn], in0=Gl.t[:, 0:tn], in1=BC.t[:, 0:tn], op=ALU.subtract),
             reads=[Gl, BC], writes=[E])
        P.op("dve", lambda e: e.tensor_tensor(out=v3(E), in0=v3(E), in1=v3(BC)[:, :, 63:64].broadcast_to([128, nch, 64]),
                                              op=ALU.add), reads=[E, BC], writes=[E])
    yield
    dl = G["dl"].get()
    P.op("act", lambda e: e.activation(out=dl.t[:, 0:nch, 0], in_=v3(BC)[:, :, 63], func=AF.Exp), reads=[BC], writes=[dl])
    P.dma("sp", S["GDL"][dh][:, t0 // 64:t0 // 64 + nch], dl.t[:, 0:nch, 0], dl, reads=[dl])
    X1 = G["X1"]
    P.op("act", lambda e: e.activation(out=X1.t[:, 0:tn], in_=Eb.t[:, 0:tn], func=AF.Exp), reads=[Eb], writes=[X1])
    X2 = G["X2"]
    P.op("act", lambda e: e.activation(out=X2.t[:, 0:tn], in_=Eb.t[:, 0:tn], func=AF.Exp, scale=-1.0), reads=[Eb], writes=[X2])
    yield
    o1 = R["b16"].get()
    P.op("dve", lambda e: e.tensor_tensor(out=o1.t[:, 0:tn], in0=QH.t[:, h, 0:tn], in1=X1.t[:, 0:tn], op=ALU.mult),
         reads=[QH, X1], writes=[o1])
    P.dma("sp", S["GQ"][dh][:, t0:t0 + tn], o1.t[:, 0:tn], o1, reads=[o1])
    P.op("dve", lambda e: e.scalar_tensor_tensor(out=Tm.t[:, 0:tn], in0=An.t[:, 0:tn], scalar=oml, in1=X2.t[:, 0:tn],
                                                 op0=ALU.mult, op1=ALU.mult), reads=[An, C.lb, X2], writes=[Tm])
    o2 = R["b16"].get()
    P.op("act", lambda e: e.copy(out=o2.t[:, 0:tn], in_=Tm.t[:, 0:tn]), reads=[Tm], writes=[o2])
    P.dma("sp", S["GK"][dh][:, t0:t0 + tn], o2.t[:, 0:tn], o2, reads=[o2])
    o3 = G["o3"]
    P.op("dve", lambda e: e.tensor_tensor(out=o3.t[:, 0:tn].rearrange("p (c i) -> p c i", i=64), in0=v3(Tm),
                                          in1=dl.t[:, 0:nch, :].broadcast_to([128, nch, 64]), op=ALU.mult),
         reads=[Tm, dl], writes=[o3])
    yield
    store_tokmajor(C, R, o3, tn, S["GKH"][dh][t0:t0 + tn, :])


class Pipe:
    def __init__(self):
        self.fl = []

    def advance(self):
        for g in list(self.fl):
            try:
                next(g)
            except StopIteration:
                self.fl.remove(g)

    def start(self, g):
        try:
            next(g)
            self.fl.append(g)
        except StopIteration:
            pass

    def drain(self):
        while self.fl:
            self.advance()


def mixer_proj_common(C, ph, l):
    P = C.P
    B = {}
    B["xT"] = P.sb("xT", [128, KC, 512], F32, ph)
    B["hT"] = P.sb("hT", [128, KC, 512], BF16, ph)
    B["wb"] = [P.sb(f"wb{j}", [128, KC, 512], BF16, ph) for j in range(2)]
    B["sq"] = [P.sb(f"sq{j}", [128, 512], F32, ph) for j in range(2)]
    B["tmp"] = [P.sb(f"tmp{j}", [128, 512], F32, ph) for j in range(2)]
    B["rstd"] = P.sb("rstd", [128, 512], F32, ph)
    B["pss"] = P.ps("pss", [128, 512], F32, ph)
    B["psp"] = [P.ps(f"psp{j}", [128, 512], F32, ph) for j in range(2)]
    B["Ct"] = P.sb("Ct", [128, 512], F32, ph)
    B["St"] = P.sb("St", [128, 512], F32, ph)
    return B


def even_proj_phase(C):
    P, I, S = C.P, C.I, C.S
    with contextlib.ExitStack() as ph:
        B = mixer_proj_common(C, ph, 0)
        R = mk_rot(C, ph)
        dlrot = Rot([P.sb(f"gdl{j}", [128, 8, 1], F32, ph) for j in range(4)])
        gA = P.sb("gA", [128, 512], F32, ph)
        gTm = P.sb("gTm", [128, 512], F32, ph)
        Gs = []
        for gj in range(3):
            Gd = {k: P.sb(f"g{k}{gj}", [128, 512], F32, ph) for k in ("F", "G", "BC", "E", "X1", "X2")}
            Gd["o3"] = P.sb(f"go3{gj}", [128, 512], BF16, ph)
            Gd["A"] = gA
            Gd["K"] = gTm
            Gd["dl"] = dlrot
            Gs.append(Gd)
        Grot = Rot(Gs)
        HF = Rot([P.sb(f"hf{j}", [128, 512], F32, ph) for j in range(3)])
        HB = Rot([P.sb(f"hb{j}", [128, 512], BF16, ph) for j in range(3)])
        pipe = Pipe()
        QH = P.sb("QH", [128, 8, 512], F32, ph)
        xT, hT, Ct, St = B["xT"], B["hT"], B["Ct"], B["St"]
        XTv = S["XT"].rearrange("(c p) t -> p c t", p=128)
        wv = S["WEB"]
        state = {"w": 0, "p": 0}
        qs = 128.0 ** -0.5
        for (t0, tn, s) in token_tiles(True):
            P.dma("sp", xT.t[:, :, 0:tn], XTv[:, :, t0:t0 + tn], xT, writes=[xT])
            P.dma("sp", Ct.t[:, 0:tn], I["ropec"][:, t0:t0 + tn], Ct, writes=[Ct])
            P.dma("sp", St.t[:, 0:tn], I["ropes"][:, t0:t0 + tn], St, writes=[St])
            modnorm(C, (B["sq"], B["pss"], B["rstd"], B["tmp"]), xT, hT, tn, 0, 1, s)

            def item(j, pb, t0=t0, tn=tn):
                if j < 8:
                    tq = HF.get()
                    P.op("act", lambda e: e.activation(out=tq.t[:, 0:tn], in_=pb.t[:, 0:tn], func=AF.Silu),
                         reads=[pb], writes=[tq])
                    yield
                    P.op("dve", lambda e: e.tensor_scalar(out=QH.t[:, j, 0:tn], in0=tq.t[:, 0:tn], scalar1=qs,
                                                          scalar2=None, op0=ALU.mult), reads=[tq], writes=[QH])
                elif j < 24:
                    yield from hgrn_gate(C, R, Grot.get(), pb, tn, (j - 8) // 8, (j - 8) % 8, QH, t0)
                elif j < 32 or j >= 56:
                    vb = HB.get()
                    P.op("act", lambda e: e.copy(out=vb.t[:, 0:tn], in_=pb.t[:, 0:tn]), reads=[pb], writes=[vb])
                    if j < 32:
                        dst = S["GV"][t0:t0 + tn, (j - 24) * 128:(j - 23) * 128]
                    else:
                        dst = S["AV"][t0:t0 + tn, (j - 56) * 128:(j - 55) * 128]
                    yield
                    store_tokmajor(C, R, vb, tn, dst)
                elif j < 40:
                    gt = R["f32"].get()
                    P.op("act", lambda e: e.activation(out=gt.t[:, 0:tn], in_=pb.t[:, 0:tn], func=AF.Sigmoid),
                         reads=[pb], writes=[gt])
                    P.dma("sp", S["GG"][(j - 32) * 128:(j - 31) * 128, t0:t0 + tn], gt.t[:, 0:tn], gt, reads=[gt])
                else:
                    isq = j < 48
                    qf = HF.get()
                    P.op("act", lambda e: e.copy(out=qf.t[:, 0:tn], in_=pb.t[:, 0:tn]), reads=[pb], writes=[qf])
                    yield
                    ob = R["b16"].get()
                    rope_op(C, R, qf, tn, qs if isq else 1.0, ob.t[:, 0:tn], ob, Ct, St)
                    dst = S["AQ"] if isq else S["AK"]
                    r0 = (j - 40) * 128 if isq else (j - 48) * 128
                    P.dma("sp", dst[r0:r0 + 128, t0:t0 + tn], ob.t[:, 0:tn], ob, reads=[ob])

            def cb(j, pb):
                pipe.advance()
                pipe.start(item(j, pb))

            proj_chunks(C, ph, hT, tn, wv, list(range(64)), B["wb"], B["psp"], cb, state)
            pipe.drain()
        P.end_phase()


def gla_scan_phase(C, nH, dv, dirn, qdir_of):
    P, S = C.P, C.S
    nvc = dv // 128
    with contextlib.ExitStack() as ph:
        St_ = [P.sb(f"S{h}", [128, dv], F32, ph) for h in range(nH)]
        Sb_ = [P.sb(f"Sb{h}", [128, dv], BF16, ph) for h in range(nH)]
        DLt = P.sb("DLt", [128, nH, NCH], F32, ph)
        LD = [[{"q": P.sb(f"lq{p}{h}", [128, 512], BF16, ph), "k": P.sb(f"lk{p}{h}", [128, 512], BF16, ph),
                "kh": P.sb(f"lkh{p}{h}", [64, 8, 128], BF16, ph), "v": P.sb(f"lv{p}{h}", [64, 8, dv], BF16, ph)}
               for h in range(nH)] for p in range(2)]
        OT = [[P.sb(f"ot{p}{h}", [128, nvc, 512], F32, ph) for h in range(nH)] for p in range(2)]
        pts = Rot([P.sb(f"pt{j}", [64, 64], BF16, ph) for j in range(4)])
        pss = Rot([P.ps(f"pss{j}", [64, 64], F32, ph) for j in range(2)])
        pso = Rot([P.ps(f"pso{j}", [128, 64], F32, ph) for j in range(3)])
        psu = Rot([P.ps(f"psu{j}", [128, dv], F32, ph) for j in range(2)])
        for h in range(nH):
            P.op("dve", lambda e, h=h: e.memset(St_[h].t[:], 0.0), writes=[St_[h]])
            P.op("dve", lambda e, h=h: e.memset(Sb_[h].t[:], 0.0), writes=[Sb_[h]])
            P.dma("sp", DLt.t[:, h, :], S["GDL"][dirn * nH + h], DLt, writes=[DLt])
        tiles = token_tiles(True)
        order = tiles if dirn == 0 else [tiles[0]] + tiles[:0:-1]
        mk = C.masks.t[:, 0:64] if dirn == 0 else C.masks.t[:, 64:128]
        for ti, (t0, tn, s) in enumerate(order):
            nch = tn // 64
            p = ti % 2
            for h in range(nH):
                L = LD[p][h]
                dq = qdir_of(dirn) * nH + h
                dk = dirn * nH + h
                P.dma("sp", L["q"].t[:, 0:tn], S["GQ"][dq][:, t0:t0 + tn], L["q"], writes=[L["q"]])
                P.dma("sp", L["k"].t[:, 0:tn], S["GK"][dk][:, t0:t0 + tn], L["k"], writes=[L["k"]])
                P.dma("sp", L["kh"].t[:, 0:nch, :], S["GKH"][dk][t0:t0 + tn, :].rearrange("(c j) k -> j c k", j=64),
                      L["kh"], writes=[L["kh"]])
                P.dma("sp", L["v"].t[:, 0:nch, :], S["GV"][t0:t0 + tn, h * dv:(h + 1) * dv].rearrange("(c j) v -> j c v", j=64),
                      L["v"], writes=[L["v"]])
            corder = range(nch) if dirn == 0 else range(nch - 1, -1, -1)
            for c in corder:
                cg = t0 // 64 + c
                for h in range(nH):
                    L = LD[p][h]
                    ps_s = pss.get()
                    P.op("pe", lambda e, L=L, c=c, ps_s=ps_s: e.matmul(
                        ps_s.t[:], lhsT=L["k"].t[:, c * 64:(c + 1) * 64], rhs=L["q"].t[:, c * 64:(c + 1) * 64],
                        start=True, stop=True), reads=[L["k"], L["q"]], writes=[ps_s])
                    pt = pts.get()
                    P.op("dve", lambda e, pt=pt, ps_s=ps_s: e.tensor_tensor(out=pt.t[:], in0=ps_s.t[:], in1=mk, op=ALU.mult),
                         reads=[ps_s, C.masks], writes=[pt])
                    for vc in range(nvc):
                        po = pso.get()
                        P.op("pe", lambda e, L=L, c=c, vc=vc, po=po, pt=pt: e.matmul(
                            po.t[:], lhsT=L["v"].t[:, c, vc * 128:(vc + 1) * 128], rhs=pt.t[:], start=True, stop=False),
                            reads=[L["v"], pt], writes=[po], inc=False)
                        P.op("pe", lambda e, L=L, c=c, vc=vc, po=po, h=h: e.matmul(
                            po.t[:], lhsT=Sb_[h].t[:, vc * 128:(vc + 1) * 128], rhs=L["q"].t[:, c * 64:(c + 1) * 64],
                            start=False, stop=True), reads=[Sb_[h], L["q"]], writes=[po])
                        P.op("act", lambda e, po=po, vc=vc, c=c, h=h, p=p: e.copy(
                            out=OT[p][h].t[:, vc, c * 64:(c + 1) * 64], in_=po.t[:]), reads=[po], writes=[OT[p][h]])
                    pu = psu.get()
                    P.op("pe", lambda e, L=L, c=c, pu=pu: e.matmul(
                        pu.t[:], lhsT=L["kh"].t[:, c, :], rhs=L["v"].t[:, c, :], start=True, stop=True),
                        reads=[L["kh"], L["v"]], writes=[pu])
                    P.op("dve", lambda e, pu=pu, h=h, cg=cg: e.scalar_tensor_tensor(
                        out=St_[h].t[:], in0=St_[h].t[:], scalar=DLt.t[:, h, cg:cg + 1], in1=pu.t[:],
                        op0=ALU.mult, op1=ALU.add), reads=[St_[h], DLt, pu], writes=[St_[h]])
                    P.op("act", lambda e, h=h: e.copy(out=Sb_[h].t[:], in_=St_[h].t[:]), reads=[St_[h]], writes=[Sb_[h]])
            for h in range(nH):
                for vc in range(nvc):
                    r0 = h * dv + vc * 128
                    P.dma("sp", S["OH"][dirn][r0:r0 + 128, t0:t0 + tn], OT[p][h].t[:, vc, 0:tn], OT[p][h], reads=[OT[p][h]])
        P.end_phase()


def headnorm_phase(C, nH, nvc, gcol, ym_row0, tiles):
    P, S = C.P, C.S
    nchunk = nH * nvc
    with contextlib.ExitStack() as ph:
        oa = [P.sb(f"oa{j}", [128, nchunk, 512], F32, ph) for j in range(2)]
        ob = [P.sb(f"ob{j}", [128, nchunk, 512], F32, ph) for j in range(2)]
        gg = [P.sb(f"gg{j}", [128, nchunk, 512], F32, ph) for j in range(2)]
        yb = [P.sb(f"yb{j}", [128, nchunk, 512], BF16, ph) for j in range(2)]
        sq = Rot([P.sb(f"sq{j}", [128, 512], F32, ph) for j in range(2)])
        rs = Rot([P.sb(f"rs{j}", [128, 512], F32, ph) for j in range(2)])
        tm = Rot([P.sb(f"tm{j}", [128, 512], F32, ph) for j in range(2)])
        pss = Rot([P.ps(f"pss{j}", [128, 512], F32, ph) for j in range(2)])
        v3 = lambda ap: ap.rearrange("(c p) t -> p c t", p=128)
        def tile_body(ti, t0, tn, s):
            a, b, g, y = oa[ti % 2], ob[ti % 2], gg[ti % 2], yb[ti % 2]
            P.dma("sp", a.t[:, :, 0:tn], v3(S["OH"][0][0:nchunk * 128, t0:t0 + tn]), a, writes=[a])
            P.dma("sp", b.t[:, :, 0:tn], v3(S["OH"][1][0:nchunk * 128, t0:t0 + tn]), b, writes=[b])
            P.dma("sp", g.t[:, :, 0:tn], v3(S["GG"][0:nchunk * 128, t0:t0 + tn]), g, writes=[g])
            for ch in range(nchunk):
                P.op("dve", lambda e, ch=ch, a=a, b=b: e.tensor_tensor(
                    out=a.t[:, ch, 0:tn], in0=a.t[:, ch, 0:tn], in1=b.t[:, ch, 0:tn], op=ALU.add), reads=[a, b], writes=[a])
            for h in range(nH):
                ps_ = pss.get()
                for vc in range(nvc):
                    ch = h * nvc + vc
                    s_ = sq.get()
                    P.op("act", lambda e, ch=ch, a=a, s_=s_: e.activation(out=s_.t[:, 0:tn], in_=a.t[:, ch, 0:tn], func=AF.Square),
                         reads=[a], writes=[s_])
                    P.op("pe", lambda e, s_=s_, ps_=ps_, vc=vc: e.matmul(
                        ps_.t[:, 0:tn], lhsT=C.ones.t[:], rhs=s_.t[:, 0:tn], start=(vc == 0), stop=(vc == nvc - 1)),
                        reads=[s_, C.ones], writes=[ps_])
                r_ = rs.get()
                P.op("act", lambda e, r_=r_, ps_=ps_: e.activation(out=r_.t[:, 0:tn], in_=ps_.t[:, 0:tn], func=AF.Ln,
                                                                 bias=C.epsb.t[:, 0:1], scale=1.0 / (128 * nvc)),
                     reads=[ps_, C.epsb], writes=[r_])
                P.op("act", lambda e, r_=r_: e.activation(out=r_.t[:, 0:tn], in_=r_.t[:, 0:tn], func=AF.Exp, scale=-0.5),
                     reads=[r_], writes=[r_])
                for vc in range(nvc):
                    ch = h * nvc + vc
                    t_ = tm.get()
                    P.op("dve", lambda e, ch=ch, vc=vc, a=a, r_=r_, t_=t_: e.scalar_tensor_tensor(
                        out=t_.t[:, 0:tn], in0=a.t[:, ch, 0:tn], scalar=C.VT.t[:, gcol + vc:gcol + vc + 1], in1=r_.t[:, 0:tn],
                        op0=ALU.mult, op1=ALU.mult), reads=[a, C.VT, r_], writes=[t_])
                    P.op("dve", lambda e, ch=ch, t_=t_, g=g, y=y: e.tensor_tensor(
                        out=y.t[:, ch, 0:tn], in0=t_.t[:, 0:tn], in1=g.t[:, ch, 0:tn], op=ALU.mult), reads=[t_, g], writes=[y])
            P.dma("sp", v3(S["YM"][ym_row0:ym_row0 + nchunk * 128, t0:t0 + tn]), y.t[:, :, 0:tn], y, reads=[y])

        for ti, (t0, tn, s) in enumerate(tiles):
            tile_body(ti, t0, tn, s)
        P.end_phase()


def attention_phase(C, units, dv, finish, extra_alloc=None):
    P = C.P
    nvc = dv // 128
    NKC = T // 128
    with contextlib.ExitStack() as ph:
        nk = max(len(u["ks"]) for u in units)
        nm = max(len(u["maps"]) for u in units)
        KT = [[P.sb(f"KT{p}{j}", [128, T], BF16, ph) for j in range(nk)] for p in range(2)]
        VV = [P.sb(f"VV{p}", [128, NKC, dv], BF16, ph) for p in range(2)]
        qb = Rot([P.sb(f"qb{j}", [128, 512], BF16, ph) for j in range(3)])
        ptb = Rot([P.sb(f"ptb{j}", [128, 512], BF16, ph) for j in range(4)])
        ob = [[P.sb(f"o{p}{m}", [128, nvc, 512], F32, ph) for m in range(nm)] for p in range(2)]
        rden = Rot([P.sb(f"rden{j}", [128, 512], F32, ph) for j in range(2)])
        pss = Rot([P.ps(f"aps{j}", [128, 512], F32, ph) for j in range(3)])
        pso = [P.ps(f"apo{j}", [128, 512], F32, ph) for j in range(nvc)]
        psd = P.ps("apd", [128, 512], F32, ph)
        X = extra_alloc(ph) if extra_alloc else None
        qi = 0
        for ui, u in enumerate(units):
            p = ui % 2
            for j, kap in enumerate(u["ks"]):
                P.dma("sp", KT[p][j].t[:], kap, KT[p][j], writes=[KT[p][j]])
            P.dma("sp", VV[p].t[:], u["v"].rearrange("(c p) v -> p c v", p=128), VV[p], writes=[VV[p]])
            def qtile_body(u, p, t0, tn, nkc, obs):
                for m, (qap, kidx) in enumerate(u["maps"]):
                    q = qb.get()
                    P.dma("sp", q.t[:, 0:tn], qap[:, t0:t0 + tn], q, writes=[q])
                    K = KT[p][kidx]

                    def s_mm(kc, K=K, q=q):
                        ps_ = pss.get()
                        P.op("pe", lambda e, K=K, kc=kc, q=q, ps_=ps_: e.matmul(
                            ps_.t[:, 0:tn], lhsT=K.t[:, kc * 128:(kc + 1) * 128], rhs=q.t[:, 0:tn], start=True, stop=True),
                            reads=[K, q], writes=[ps_])
                        return ps_

                    LOOK = 2
                    pend = [s_mm(kc) for kc in range(min(LOOK, nkc))]
                    for kc in range(nkc):
                        if kc + LOOK < nkc:
                            pend.append(s_mm(kc + LOOK))
                        ps_ = pend.pop(0)
                        pt = ptb.get()
                        P.op("act", lambda e, pt=pt, ps_=ps_: e.activation(out=pt.t[:, 0:tn], in_=ps_.t[:, 0:tn], func=AF.Exp),
                             reads=[ps_], writes=[pt])
                        for vc in range(nvc):
                            P.op("pe", lambda e, vc=vc, kc=kc, pt=pt, p=p: e.matmul(
                                pso[vc].t[:, 0:tn], lhsT=VV[p].t[:, kc, vc * 128:(vc + 1) * 128], rhs=pt.t[:, 0:tn],
                                start=(kc == 0), stop=(kc == nkc - 1)), reads=[VV[p], pt], writes=[pso[vc]], inc=False)
                        P.op("pe", lambda e, kc=kc, pt=pt: e.matmul(
                            psd.t[:, 0:tn], lhsT=C.onesb.t[:], rhs=pt.t[:, 0:tn], start=(kc == 0), stop=(kc == nkc - 1)),
                            reads=[C.onesb, pt], writes=[psd])
                    rd = rden.get()
                    P.op("dve", lambda e, rd=rd: e.reciprocal(out=rd.t[:, 0:tn], in_=psd.t[:, 0:tn]), reads=[psd], writes=[rd])
                    for vc in range(nvc):
                        P.op("dve", lambda e, vc=vc, rd=rd, m=m, obs=obs: e.tensor_tensor(
                            out=obs[m].t[:, vc, 0:tn], in0=pso[vc].t[:, 0:tn], in1=rd.t[:, 0:tn], op=ALU.mult),
                            reads=[pso[vc], rd], writes=[obs[m]])
                finish(u, t0, tn, obs, X)

            for (t0, tn, nkc) in u["qtiles"]:
                obs = ob[qi % 2]
                qi += 1
                qtile_body(u, p, t0, tn, nkc, obs)
        P.end_phase()


def out_proj_phase(C, l, w_out, tiles):
    P, I, S = C.P, C.I, C.S
    with contextlib.ExitStack() as ph:
        W = P.sb("wo", [128, KC, D], BF16, ph)
        ym = [P.sb(f"ym{j}", [128, KC, 512], BF16, ph) for j in range(2)]
        xT = [P.sb(f"oxT{j}", [128, KC, 512], F32, ph) for j in range(2)]
        psy = Rot([P.ps(f"opy{j}", [128, 512], F32, ph) for j in range(3)])
        wv = w_out.rearrange("(kc p) n -> p kc n", p=128)
        for hh in range(4):
            P.dma("pool", W.t[:, :, hh * 512:(hh + 1) * 512], wv[:, :, hh * 512:(hh + 1) * 512], W, writes=[W])
        XTv = S["XT"].rearrange("(c p) t -> p c t", p=128)
        YMv = S["YM"].rearrange("(c p) t -> p c t", p=128)
        def tile_body(ti, t0, tn, s):
            y, x = ym[ti % 2], xT[ti % 2]
            P.dma("sp", y.t[:, :, 0:tn], YMv[:, :, t0:t0 + tn], y, writes=[y])
            P.dma("sp", x.t[:, :, 0:tn], XTv[:, :, t0:t0 + tn], x, writes=[x])
            for dc in range(KC):
                py = psy.get()
                for kc in range(KC):
                    P.op("pe", lambda e, py=py, kc=kc, dc=dc, y=y: e.matmul(
                        py.t[:, 0:tn], lhsT=W.t[:, kc, dc * 128:(dc + 1) * 128], rhs=y.t[:, kc, 0:tn],
                        start=(kc == 0), stop=(kc == KC - 1)), reads=[W, y], writes=[py], inc=(kc == KC - 1))
                P.op("dve", lambda e, py=py, dc=dc, x=x, s=s: e.scalar_tensor_tensor(
                    out=x.t[:, dc, 0:tn], in0=py.t[:, 0:tn], scalar=C.HG.t[:, l, 1, s, dc:dc + 1], in1=x.t[:, dc, 0:tn],
                    op0=ALU.mult, op1=ALU.add), reads=[py, C.HG, x], writes=[x])
            P.dma("sp", XTv[:, :, t0:t0 + tn], x.t[:, :, 0:tn], x, reads=[x])

        for ti, (t0, tn, s) in enumerate(tiles):
            tile_body(ti, t0, tn, s)
        P.end_phase()


def even_mixer(C):
    P, S = C.P, C.S
    even_proj_phase(C)
    if C.stop == "m0p":
        return
    gla_scan_phase(C, 8, 128, 0, lambda d: d)
    gla_scan_phase(C, 8, 128, 1, lambda d: d)
    if C.stop == "m0s":
        return
    headnorm_phase(C, 8, 1, R_HGG, 0, token_tiles(True))
    lat_q = [(NCTX + 512 * i, 512, T // 128) for i in range(8)]
    units = []
    for hh in range(4):
        units.append(dict(ks=[S["AK"][(2 * hh) * 128:(2 * hh + 1) * 128, :], S["AK"][(2 * hh + 1) * 128:(2 * hh + 2) * 128, :]],
                          maps=[(S["AQ"][(2 * hh) * 128:(2 * hh + 1) * 128, :], 0), (S["AQ"][(2 * hh + 1) * 128:(2 * hh + 2) * 128, :], 1)],
                          v=S["AV"][:, hh * 256:(hh + 1) * 256], qtiles=[(0, NCTX, NCTX // 128)] + lat_q, hh=hh))

    def alloc(ph):
        X = {}
        X["dd"] = Rot([P.sb(f"dd{j}", [128, 2, 512], F32, ph) for j in range(2)])
        X["sq"] = Rot([P.sb(f"dsq{j}", [128, 512], F32, ph) for j in range(2)])
        X["rs"] = Rot([P.sb(f"drs{j}", [128, 512], F32, ph) for j in range(2)])
        X["tm"] = Rot([P.sb(f"dtm{j}", [128, 512], F32, ph) for j in range(2)])
        X["yb"] = Rot([P.sb(f"dyb{j}", [128, 2, 512], BF16, ph) for j in range(2)])
        X["ps"] = P.ps("dps", [128, 512], F32, ph)
        return X

    def finish(u, t0, tn, obs, X):
        hh = u["hh"]
        dd = X["dd"].get()
        for vc in range(2):
            P.op("dve", lambda e, vc=vc: e.scalar_tensor_tensor(
                out=dd.t[:, vc, 0:tn], in0=obs[1].t[:, vc, 0:tn], scalar=C.small.t[:, 0:1], in1=obs[0].t[:, vc, 0:tn],
                op0=ALU.mult, op1=ALU.add), reads=[obs[0], obs[1], C.small], writes=[dd])
        ps_ = X["ps"]
        for vc in range(2):
            s_ = X["sq"].get()
            P.op("act", lambda e, vc=vc, s_=s_: e.activation(out=s_.t[:, 0:tn], in_=dd.t[:, vc, 0:tn], func=AF.Square),
                 reads=[dd], writes=[s_])
            P.op("pe", lambda e, vc=vc, s_=s_: e.matmul(ps_.t[:, 0:tn], lhsT=C.ones.t[:], rhs=s_.t[:, 0:tn],
                                                       start=(vc == 0), stop=(vc == 1)), reads=[s_, C.ones], writes=[ps_])
        r_ = X["rs"].get()
        P.op("act", lambda e: e.activation(out=r_.t[:, 0:tn], in_=ps_.t[:, 0:tn], func=AF.Ln, bias=C.epsb.t[:, 0:1], scale=1.0 / 256),
             reads=[ps_, C.epsb], writes=[r_])
        P.op("act", lambda e: e.activation(out=r_.t[:, 0:tn], in_=r_.t[:, 0:tn], func=AF.Exp, scale=-0.5), reads=[r_], writes=[r_])
        yb = X["yb"].get()
        for vc in range(2):
            t_ = X["tm"].get()
            P.op("dve", lambda e, vc=vc, t_=t_: e.scalar_tensor_tensor(
                out=t_.t[:, 0:tn], in0=dd.t[:, vc, 0:tn], scalar=C.VT.t[:, R_DAG + vc:R_DAG + vc + 1], in1=r_.t[:, 0:tn],
                op0=ALU.mult, op1=ALU.mult), reads=[dd, C.VT, r_], writes=[t_])
            P.op("dve", lambda e, vc=vc, t_=t_: e.tensor_scalar(
                out=yb.t[:, vc, 0:tn], in0=t_.t[:, 0:tn], scalar1=float(1.0 - C.lam_init), scalar2=None, op0=ALU.mult),
                reads=[t_], writes=[yb])
        r0 = 1024 + hh * 256
        P.dma("sp", S["YM"][r0:r0 + 256, t0:t0 + tn].rearrange("(c p) t -> p c t", p=128), yb.t[:, :, 0:tn], yb, reads=[yb])

    attention_phase(C, units, 256, finish, alloc)
    if C.stop == "m0a":
        return
    out_proj_phase(C, 0, C.I["even_w_out"], token_tiles(True))


def qknorm(C, R, B, pb, tn, gcol):
    P = C.P
    qf = R["q_qf"].get()
    P.op("act", lambda e: e.copy(out=qf.t[:, 0:tn], in_=pb.t[:, 0:tn]), reads=[pb], writes=[qf])
    s_ = R["q_sq"].get()
    P.op("act", lambda e: e.activation(out=s_.t[:, 0:tn], in_=pb.t[:, 0:tn], func=AF.Square), reads=[pb], writes=[s_])
    yield
    ps_ = B["pss"]
    P.op("pe", lambda e: e.matmul(ps_.t[:, 0:tn], lhsT=C.ones.t[:], rhs=s_.t[:, 0:tn], start=True, stop=True),
         reads=[s_, C.ones], writes=[ps_])
    yield
    r_ = R["q_r"].get()
    P.op("act", lambda e: e.activation(out=r_.t[:, 0:tn], in_=ps_.t[:, 0:tn], func=AF.Ln, bias=C.epsb.t[:, 0:1], scale=1.0 / 128),
         reads=[ps_, C.epsb], writes=[r_])
    P.op("act", lambda e: e.activation(out=r_.t[:, 0:tn], in_=r_.t[:, 0:tn], func=AF.Exp, scale=-0.5), reads=[r_], writes=[r_])
    yield
    qn = R["q_qn"].get()
    P.op("dve", lambda e: e.scalar_tensor_tensor(out=qn.t[:, 0:tn], in0=qf.t[:, 0:tn], scalar=C.VT.t[:, gcol:gcol + 1],
                                                 in1=r_.t[:, 0:tn], op0=ALU.mult, op1=ALU.mult),
         reads=[qf, C.VT, r_], writes=[qn])
    return qn


def odd_proj_phase(C):
    P, I, S = C.P, C.I, C.S
    with contextlib.ExitStack() as ph:
        B = mixer_proj_common(C, ph, 1)
        R = mk_rot(C, ph)
        R["q_qf"] = Rot([P.sb(f"qqf{j}", [128, 512], F32, ph) for j in range(4)])
        R["q_sq"] = Rot([P.sb(f"qsq{j}", [128, 512], F32, ph) for j in range(2)])
        R["q_r"] = Rot([P.sb(f"qr{j}", [128, 512], F32, ph) for j in range(2)])
        R["q_qn"] = Rot([P.sb(f"qqn{j}", [128, 512], F32, ph) for j in range(2)])
        HF = Rot([P.sb(f"hf{j}", [128, 512], F32, ph) for j in range(3)])
        HB = Rot([P.sb(f"hb{j}", [128, 512], BF16, ph) for j in range(3)])
        QR = P.sb("QR", [128, 4, 512], F32, ph)
        KRrot = Rot([P.sb(f"krr{j}", [128, 512], F32, ph) for j in range(2)])
        O3rot = Rot([P.sb(f"o3r{j}", [128, 512], BF16, ph) for j in range(4)])
        pipe = Pipe()
        EX = P.sb("EX", [128, 8, 2, 512], F32, ph)
        dls = P.sb("dls", [128, 8], F32, ph)
        I1 = P.sb("I1", [128, 512], F32, ph)
        I2 = P.sb("I2", [128, 512], F32, ph)
        dlc = P.sb("dlc", [128, NCH], F32, ph)
        sm = C.small
        P.op("pool", lambda e: e.iota(I1.t[:].rearrange("p (c i) -> p c i", i=64), pattern=[[0, 8], [1, 64]], base=1,
                                      channel_multiplier=0, allow_small_or_imprecise_dtypes=True), writes=[I1])
        P.op("dve", lambda e: e.tensor_scalar(out=I2.t[:], in0=I1.t[:], scalar1=-1.0, scalar2=65.0, op0=ALU.mult, op1=ALU.add),
             reads=[I1], writes=[I2])
        for dirn in range(2):
            Ix, Iy = (I1, I2) if dirn == 0 else (I2, I1)
            for h in range(4):
                dh = dirn * 4 + h
                lg = sm.t[:, 8 + dh:9 + dh]
                nlg = sm.t[:, 16 + dh:17 + dh]
                P.op("act", lambda e, dh=dh, lg=lg, Ix=Ix: e.activation(out=EX.t[:, dh, 0, :], in_=Ix.t[:], func=AF.Exp, scale=lg),
                     reads=[Ix, sm], writes=[EX])
                P.op("act", lambda e, dh=dh, nlg=nlg, Ix=Ix: e.activation(out=EX.t[:, dh, 1, :], in_=Ix.t[:], func=AF.Exp, scale=nlg),
                     reads=[Ix, sm], writes=[EX])
                P.op("act", lambda e, dh=dh: e.activation(out=dls.t[:, dh:dh + 1], in_=sm.t[:, 8 + dh:9 + dh], func=AF.Exp, scale=64.0),
                     reads=[sm], writes=[dls])
                P.op("dve", lambda e: e.memset(dlc.t[:], 64.0), writes=[dlc])
                P.op("act", lambda e, lg=lg: e.activation(out=dlc.t[:], in_=dlc.t[:], func=AF.Exp, scale=lg), reads=[dlc, sm], writes=[dlc])
                P.dma("sp", S["GDL"][dh], dlc.t[:], dlc, reads=[dlc])
        xT, hT, Ct, St = B["xT"], B["hT"], B["Ct"], B["St"]
        XTv = S["XT"].rearrange("(c p) t -> p c t", p=128)
        wv = S["WOB"]
        state = {"w": 0, "p": 0}
        qs = 128.0 ** -0.5
        for (t0, tn, s) in token_tiles(True):
            P.dma("sp", xT.t[:, :, 0:tn], XTv[:, :, t0:t0 + tn], xT, writes=[xT])
            P.dma("sp", Ct.t[:, 0:tn], I["ropec"][:, t0:t0 + tn], Ct, writes=[Ct])
            P.dma("sp", St.t[:, 0:tn], I["ropes"][:, t0:t0 + tn], St, writes=[St])
            modnorm(C, (B["sq"], B["pss"], B["rstd"], B["tmp"]), xT, hT, tn, 1, 1, s)

            def item(j, pb, t0=t0, tn=tn):
                if j < 10:
                    isq = j < 8
                    qn = yield from qknorm(C, R, B, pb, tn, R_QG if isq else R_KG)
                    yield
                    ob = R["b16"].get()
                    rope_op(C, R, qn, tn, qs if isq else 1.0, ob.t[:, 0:tn], ob, Ct, St)
                    dst = S["BQ"] if isq else S["BK"]
                    r0 = j * 128 if isq else (j - 8) * 128
                    P.dma("sp", dst[r0:r0 + 128, t0:t0 + tn], ob.t[:, 0:tn], ob, reads=[ob])
                elif j < 12 or (20 <= j < 28):
                    vb = HB.get()
                    P.op("act", lambda e: e.copy(out=vb.t[:, 0:tn], in_=pb.t[:, 0:tn]), reads=[pb], writes=[vb])
                    if j < 12:
                        dst = S["BV"][t0:t0 + tn, (j - 10) * 128:(j - 9) * 128]
                    else:
                        dst = S["GV"][t0:t0 + tn, (j - 20) * 128:(j - 19) * 128]
                    yield
                    store_tokmajor(C, R, vb, tn, dst)
                elif j < 16:
                    h = j - 12
                    qf = HF.get()
                    P.op("act", lambda e: e.copy(out=qf.t[:, 0:tn], in_=pb.t[:, 0:tn]), reads=[pb], writes=[qf])
                    yield
                    rope_op(C, R, qf, tn, 1.0, QR.t[:, h, 0:tn], QR, Ct, St)
                elif j < 20:
                    h = j - 16
                    kf = HF.get()
                    P.op("act", lambda e: e.copy(out=kf.t[:, 0:tn], in_=pb.t[:, 0:tn]), reads=[pb], writes=[kf])
                    yield
                    kr = KRrot.get()
                    rope_op(C, R, kf, tn, qs, kr.t[:, 0:tn], kr, Ct, St)
                    o3s = []
                    for dirn in range(2):
                        dh = dirn * 4 + h
                        o1 = R["b16"].get()
                        P.op("dve", lambda e, dh=dh, o1=o1: e.tensor_tensor(out=o1.t[:, 0:tn], in0=QR.t[:, h, 0:tn],
                                                                            in1=EX.t[:, dh, 0, 0:tn], op=ALU.mult),
                             reads=[QR, EX], writes=[o1])
                        P.dma("sp", S["GQ"][dh][:, t0:t0 + tn], o1.t[:, 0:tn], o1, reads=[o1])
                        o2 = R["b16"].get()
                        P.op("dve", lambda e, dh=dh, o2=o2: e.tensor_tensor(out=o2.t[:, 0:tn], in0=kr.t[:, 0:tn],
                                                                            in1=EX.t[:, dh, 1, 0:tn], op=ALU.mult),
                             reads=[kr, EX], writes=[o2])
                        P.dma("sp", S["GK"][dh][:, t0:t0 + tn], o2.t[:, 0:tn], o2, reads=[o2])
                        o3 = O3rot.get()
                        P.op("dve", lambda e, dh=dh, o3=o3: e.scalar_tensor_tensor(
                            out=o3.t[:, 0:tn], in0=kr.t[:, 0:tn], scalar=dls.t[:, dh:dh + 1], in1=EX.t[:, dh, 1, 0:tn],
                            op0=ALU.mult, op1=ALU.mult), reads=[kr, EX, dls], writes=[o3])
                        o3s.append((dh, o3))
                    yield
                    for dh, o3 in o3s:
                        store_tokmajor(C, R, o3, tn, S["GKH"][dh][t0:t0 + tn, :])
                else:
                    gt = R["f32"].get()
                    P.op("act", lambda e: e.activation(out=gt.t[:, 0:tn], in_=pb.t[:, 0:tn], func=AF.Silu), reads=[pb], writes=[gt])
                    P.dma("sp", S["GG"][(j - 28) * 128:(j - 27) * 128, t0:t0 + tn], gt.t[:, 0:tn], gt, reads=[gt])

            def cb(j, pb):
                pipe.advance()
                pipe.start(item(j, pb))

            proj_chunks(C, ph, hT, tn, wv, list(range(36)), B["wb"], B["psp"], cb, state)
            pipe.drain()
        P.end_phase()


def odd_mixer(C):
    P, S = C.P, C.S
    odd_proj_phase(C)
    if C.stop == "m1p":
        return
    gla_scan_phase(C, 4, 256, 0, lambda d: d)
    gla_scan_phase(C, 4, 256, 1, lambda d: d)
    headnorm_phase(C, 4, 2, R_RG, 1024, token_tiles(False))
    lat_q = [(NCTX + 512 * i, 512, T // 128) for i in range(8)]
    units = []
    for kv in range(2):
        units.append(dict(ks=[S["BK"][kv * 128:(kv + 1) * 128, :]],
                          maps=[(S["BQ"][(kv * 4 + m) * 128:(kv * 4 + m + 1) * 128, :], 0) for m in range(4)],
                          v=S["BV"][:, kv * 128:(kv + 1) * 128], qtiles=lat_q, kv=kv))

    def alloc(ph):
        return {"yb": Rot([P.sb(f"gyb{j}", [128, 512], BF16, ph) for j in range(4)])}

    def finish(u, t0, tn, obs, X):
        for m in range(4):
            yb = X["yb"].get()
            P.op("act", lambda e, m=m, yb=yb: e.copy(out=yb.t[:, 0:tn], in_=obs[m].t[:, 0, 0:tn]), reads=[obs[m]], writes=[yb])
            r0 = (u["kv"] * 4 + m) * 128
            P.dma("sp", S["YM"][r0:r0 + 128, t0:t0 + tn], yb.t[:, 0:tn], yb, reads=[yb])

    attention_phase(C, units, 128, finish, alloc)
    if C.stop == "m1a":
        return
    out_proj_phase(C, 1, C.I["odd_w_out"], token_tiles(False))


def host_consts():
    import ml_dtypes
    rows = SEQ // 64
    row = np.repeat(np.arange(rows), 64)
    col = np.tile(np.arange(64), rows)
    inv = (10000.0 ** (-np.arange(32, dtype=np.float32) / 32)).astype(np.float32)
    pos = np.stack([row, col], 0).astype(np.float32)
    ang = pos[:, None, :] * inv[None, :, None]
    cosv = np.cos(ang).astype(np.float32)
    sinv = np.sin(ang).astype(np.float32)
    ropec = np.ones((128, T), np.float32)
    ropes = np.zeros((128, T), np.float32)
    for ax in range(2):
        for half in range(2):
            p0 = ax * 64 + half * 32
            ropec[p0:p0 + 32, NCTX:] = cosv[ax]
            ropes[p0:p0 + 32, NCTX:] = sinv[ax]
    rmat = np.zeros((128, 128), np.float32)
    for ax in range(2):
        for f in range(32):
            p1 = ax * 64 + f
            p2 = ax * 64 + 32 + f
            rmat[p2, p1] = -1.0
            rmat[p1, p2] = 1.0
    jj, ii = np.meshgrid(np.arange(64), np.arange(64), indexing="ij")
    masks = np.concatenate([(jj <= ii), (jj >= ii)], axis=1).astype(np.float32)
    return ropec, ropes, rmat, masks


_NC_CACHE = {}


def make_in_maps(inp):
    ropec, ropes, rmat, masks = host_consts()
    f = lambda a: np.ascontiguousarray(np.asarray(a, dtype=np.float32))
    shared = {
        "mod_w": f(inp["mod_w"]), "ffn_w13": f(inp["ffn_w13"]), "ffn_w2": f(inp["ffn_w2"]),
        "even_w_in": f(inp["even_w_in"][0]), "even_w_out": f(inp["even_w_out"][0]),
        "odd_w_in": f(inp["odd_w_in"][0]), "odd_w_out": f(inp["odd_w_out"][0]),
        "ropec": ropec, "ropes": ropes, "rmat": rmat, "masks": masks,
        "lamrep": f(np.broadcast_to(np.asarray(inp["diff_lambda"][0]).reshape(1, 512), (128, 512))),
        "decrep": f(np.broadcast_to(np.asarray(inp["ret_decay_logit"][0]).reshape(1, 8), (128, 8))),
    }
    maps = []
    for b in range(8):
        vecs = np.zeros((R_TOT, 128), np.float32)
        vecs[R_C:R_C + 16] = np.asarray(inp["c"][b]).reshape(16, 128)
        vecs[R_CCTX:R_CCTX + 16] = np.asarray(inp["c_ctx"]).reshape(16, 128)
        vecs[R_NG:R_NG + 96] = np.asarray(inp["norm_g"]).reshape(96, 128)
        vecs[R_FG:R_FG + 16] = np.asarray(inp["final_norm_g"]).reshape(16, 128)
        vecs[R_LB:R_LB + 24] = np.asarray(inp["hgrn_lb_logits"]).reshape(24, 128)
        vecs[R_HGG] = np.asarray(inp["hgrn_norm_g"][0])
        vecs[R_DAG:R_DAG + 2] = np.asarray(inp["diff_norm_g"][0]).reshape(2, 128)
        vecs[R_QG] = np.asarray(inp["gqa_q_norm_g"][0])
        vecs[R_KG] = np.asarray(inp["gqa_k_norm_g"][0])
        vecs[R_RG:R_RG + 2] = np.asarray(inp["ret_norm_g"][0]).reshape(2, 128)
        vecs[R_MB:R_MB + 288] = np.asarray(inp["mod_b"]).reshape(288, 128)
        m = dict(shared)
        m["x"] = f(inp["x"][b])
        m["ctx"] = f(inp["ctx"][b])
        m["vecs"] = vecs
        maps.append(m)
    return maps


def kernel(**inp):
    if "nc" not in _NC_CACHE:
        _NC_CACHE["nc"] = build_program()
    nc = _NC_CACHE["nc"]
    maps = make_in_maps(inp)
    res = run_bass_kernel_spmd(nc, maps, core_ids=list(range(8)))
    return np.stack([np.asarray(r["out"], dtype=np.float32) for r in res.results], axis=0)
```

```python
import contextlib
import math
import numpy as np
import concourse.bass as bass
import concourse.mybir as mybir
from concourse.bass_utils import run_bass_kernel_spmd

F32 = mybir.dt.float32
BF16 = mybir.dt.bfloat16
AF = mybir.ActivationFunctionType
ALU = mybir.AluOpType
AX = mybir.AxisListType

D = 2048
KC = 16
SEQ = 4096
NCTX = 256
T = SEQ + NCTX
DFF = 5632
FC = DFF // 128
EPS = 1e-6
NCH = T // 64
EVEN_IN = 8192
SELF_RAW_DIST = 1000000
ODD_IN = 4608

R_C, R_CCTX, R_NG = 0, 16, 32
R_FG = 128
R_LB = 144
R_HGG = 168
R_DAG = 169
R_QG, R_KG = 171, 172
R_RG = 173
R_MB = 256
R_TOT = 640


class Buf:
    __slots__ = ("t", "w", "r", "dsem", "name")

    def __init__(self, t, name):
        self.t = t
        self.w = None
        self.r = {}
        self.dsem = None
        self.name = name


class Prog:
    CE = ("pe", "act", "dve", "pool")
    ENG = ("pe", "act", "dve", "pool", "sp")

    def __init__(self, nc, stack):
        self.nc = nc
        self.stack = stack
        self.esem = {e: stack.enter_context(nc.semaphore("es_" + e)) for e in self.CE}
        self.cnt = {e: 0 for e in self.CE}
        self.ops = {e: [] for e in self.ENG}
        self.known = {e: {} for e in self.ENG}
        self.dsems = []
        self.dtot = []
        self.dfree = []
        self.phase_bufs = []
        self.uid = 0
        self.nops = 0

    def sb(self, name, shape, dt, st=None):
        self.uid += 1
        t = (st or self.stack).enter_context(self.nc.sbuf_tensor(f"{name}_{self.uid}", list(shape), dt))
        b = Buf(t, name)
        if st is not None:
            self.phase_bufs.append(b)
        return b

    def ps(self, name, shape, dt, st):
        self.uid += 1
        t = st.enter_context(self.nc.psum_tensor(f"{name}_{self.uid}", list(shape), dt))
        return Buf(t, name)

    def _dsem(self, b):
        if b.dsem is None:
            if self.dfree:
                b.dsem = self.dfree.pop()
            else:
                s = self.stack.enter_context(self.nc.semaphore(f"ds{len(self.dsems)}"))
                self.dsems.append(s)
                self.dtot.append(0)
                b.dsem = len(self.dsems) - 1
        return b.dsem

    def _need(self, eng, waits, ev, raw=False):
        if ev is None:
            return
        k, v = ev
        if k == eng and eng != "pool":
            if eng == "pe" or not raw or v <= self.cnt[eng] - SELF_RAW_DIST:
                return
        if self.known[eng].get(k, 0) >= v:
            return
        if waits.get(k, 0) < v:
            waits[k] = v

    def _deps(self, eng, reads, writes):
        waits = {}
        for b in reads:
            self._need(eng, waits, b.w, raw=True)
        for b in writes:
            self._need(eng, waits, b.w)
            for k, v in b.r.items():
                self._need(eng, waits, (k, v))
        for k, v in waits.items():
            self.known[eng][k] = v
        return tuple(waits.items())

    def op(self, eng, fn, reads=(), writes=(), inc=True):
        waits = self._deps(eng, reads, writes)
        seq = self.cnt[eng] + 1
        if inc:
            self.cnt[eng] = seq
        self.ops[eng].append((waits, fn, 1 if inc else 0))
        self.nops += 1
        for b in reads:
            if b not in writes:
                b.r[eng] = seq
        for b in writes:
            b.w = (eng, seq)
            b.r = {}

    def dma(self, q, out_ap, in_ap, sbuf, reads=(), writes=()):
        waits = self._deps(q, reads, writes)
        di = self._dsem(sbuf)
        self.dtot[di] += 16
        key, val = ("d", di), self.dtot[di]
        self.ops[q].append((waits, (out_ap, in_ap), ("dma", di)))
        self.nops += 1
        for b in reads:
            b.r[key] = val
        for b in writes:
            b.w = (key, val)
            b.r = {}

    def barrier(self):
        for e in self.ENG:
            waits = {}
            for k in self.CE:
                if k != e and self.cnt[k] > 0:
                    self._need(e, waits, (k, self.cnt[k]))
            for i, tot in enumerate(self.dtot):
                if tot > 0:
                    self._need(e, waits, (("d", i), tot))
            for k, v in waits.items():
                self.known[e][k] = v
            if waits:
                self.ops[e].append((tuple(waits.items()), None, 0))

    def end_phase(self):
        self.barrier()
        for b in self.phase_bufs:
            if b.dsem is not None:
                self.dfree.append(b.dsem)
                b.dsem = None
        self.phase_bufs = []
        self.flush()

    def _sem(self, k):
        return self.dsems[k[1]] if isinstance(k, tuple) else self.esem[k]

    def _emit(self, name, e):
        for waits, fn, inc in self.ops[name]:
            for k, v in waits:
                e.wait_ge(self._sem(k), v)
            if fn is None:
                continue
            if isinstance(inc, tuple):
                o, i = fn
                e.dma_start(out=o, in_=i).then_inc(self.dsems[inc[1]], 16)
            else:
                ins = fn(e)
                if inc:
                    ins.then_inc(self.esem[name], 1)

    def flush(self):
        self.nflush = getattr(self, "nflush", 0) + 1
        with self.nc.named_scope(f"ph{self.nflush:02d}"), self.nc.Block() as block:
            @block.tensor
            def _(e):
                self._emit("pe", e)

            @block.scalar
            def _(e):
                self._emit("act", e)

            @block.vector
            def _(e):
                self._emit("dve", e)

            @block.gpsimd
            def _(e):
                self._emit("pool", e)

            @block.sync
            def _(e):
                self._emit("sp", e)
        self.ops = {e: [] for e in self.ENG}


def token_tiles(with_ctx=True):
    tl = [(0, NCTX, 1)] if with_ctx else []
    for i in range(SEQ // 512):
        tl.append((NCTX + 512 * i, 512, 0))
    return tl


class Ctx:
    pass


def build_program(debug=None):
    nc = bass.Bass("TRN2", target_bir_lowering=False)
    I = {}

    def din(name, shape, dt=F32):
        I[name] = nc.dram_tensor(name, list(shape), dt, kind="ExternalInput").ap()
        return I[name]

    din("x", [SEQ, D]); din("ctx", [NCTX, D]); din("vecs", [R_TOT, 128])
    din("lamrep", [128, 512]); din("decrep", [128, 8])
    din("mod_w", [2, D, 9 * D]); din("ffn_w13", [2, 2, D, 2 * DFF]); din("ffn_w2", [2, 2, DFF, D])
    din("even_w_in", [D, EVEN_IN]); din("even_w_out", [D, D])
    din("odd_w_in", [D, ODD_IN]); din("odd_w_out", [D, D])
    din("ropec", [128, T]); din("ropes", [128, T]); din("rmat", [128, 128])
    din("masks", [64, 128])
    out = nc.dram_tensor("out", [SEQ, D], F32, kind="ExternalOutput").ap()

    S = {}

    def scratch(name, shape, dt):
        kind = "ExternalOutput" if (debug and name in debug) else "Internal"
        S[name] = nc.dram_tensor("s_" + name, list(shape), dt, kind=kind).ap()
        return S[name]

    scratch("XT", [D, T], F32)
    scratch("W13B", [2, 2, 44, 128, 4096], BF16)
    scratch("W2B", [2, 2, 16, 128, 5632], BF16)
    scratch("WEB", [16, 128, 8192], BF16)
    scratch("WOB", [9, 128, 8192], BF16)
    if debug and "LBD" in debug:
        scratch("LBD", [4, 128, 96], F32)
    if debug and "HD" in debug:
        scratch("HD", [5, 128, 512], F32)
    if debug and "MTD" in debug:
        scratch("MTD", [128, 576 + 192 + 192], F32)
    scratch("YM", [D, T], BF16)
    scratch("GQ", [16, 128, T], BF16); scratch("GK", [16, 128, T], BF16)
    scratch("GKH", [16, T, 128], BF16); scratch("GV", [T, 1024], BF16)
    scratch("GDL", [16, 128, NCH], F32); scratch("GG", [1024, T], F32)
    scratch("OH", [2, 1024, T], F32)
    scratch("AQ", [1024, T], BF16); scratch("AK", [1024, T], BF16); scratch("AV", [T, 1024], BF16)
    scratch("BQ", [1024, T], BF16); scratch("BK", [256, T], BF16); scratch("BV", [T, 256], BF16)

    with contextlib.ExitStack() as stack:
        P = Prog(nc, stack)
        C = Ctx()
        C.nc, C.P, C.I, C.S, C.out = nc, P, I, S, out
        C.stop = debug.get("stop") if debug else None
        setup_phase(C)
        stop = debug.get("stop") if debug else None
        if stop == "setup":
            return nc
        phases = [
            ("pre", lambda: precast_phase(C)),
            ("in", lambda: input_phase(C)),
            ("f00", lambda: ffn_phase(C, 0, 0, token_tiles(True))),
            ("m0", lambda: even_mixer(C)),
            ("f01", lambda: ffn_phase(C, 0, 1, token_tiles(True))),
            ("f10", lambda: ffn_phase(C, 1, 0, token_tiles(True))),
            ("m1", lambda: odd_mixer(C)),
            ("f11", lambda: ffn_phase(C, 1, 1, token_tiles(False))),
            ("fin", lambda: final_phase(C)),
        ]
        def lbdump(ii):
            if "LBD" in S:
                P.dma("sp", S["LBD"][ii][:, 0:64], C.small.t[:], C.small, reads=[C.small])
                P.dma("sp", S["LBD"][ii][:, 64:80], C.lb.t[:], C.lb, reads=[C.lb])
        for pi, (name, fn) in enumerate(phases):
            if pi < 4:
                lbdump(pi)
            fn()
            if stop is not None and stop.startswith(name):
                break
    return nc


def setup_phase(C):
    nc, P, I = C.nc, C.P, C.I
    g = P.stack
    C.ident = P.sb("ident", [128, 128], F32)
    C.identb = P.sb("identb", [128, 128], BF16)
    C.ones = P.sb("ones", [128, 128], F32)
    C.onesb = P.sb("onesb", [128, 128], BF16)
    C.ones512 = P.sb("ones512", [128, 512], F32)
    C.VT = P.sb("VT", [128, R_TOT], F32)
    C.MT = P.sb("MT", [128, 2, 144, 2], F32)
    C.GS = P.sb("GS", [128, 2, 3, 2, 16], F32)
    C.HG = P.sb("HG", [128, 2, 3, 2, 16], F32)
    C.rmat = P.sb("rmatf", [128, 128], F32)
    C.masks = P.sb("masks", [64, 128], F32)
    C.small = P.sb("small", [128, 64], F32)
    C.lb = P.sb("lb", [128, 16], F32)
    C.epsb = P.sb("epsb", [128, 1], F32)
    C.rmask = P.sb("rmask", [128, 512], F32)

    with contextlib.ExitStack() as ph:
        P.op("pool", lambda e: e.iota(C.ident.t[:], pattern=[[1, 128]], base=0, channel_multiplier=-1,
                                      allow_small_or_imprecise_dtypes=True), writes=[C.ident])
        P.op("pool", lambda e: e.tensor_single_scalar(out=C.ident.t[:], in_=C.ident.t[:], scalar=0.0,
                                                      op=ALU.is_equal), reads=[C.ident], writes=[C.ident])
        P.op("dve", lambda e: e.tensor_copy(out=C.identb.t[:], in_=C.ident.t[:]), reads=[C.ident], writes=[C.identb])
        P.op("dve", lambda e: e.memset(C.ones.t[:], 1.0), writes=[C.ones])
        P.op("dve", lambda e: e.memset(C.onesb.t[:], 1.0), writes=[C.onesb])
        P.op("dve", lambda e: e.memset(C.ones512.t[:], 1.0), writes=[C.ones512])
        P.op("dve", lambda e: e.memset(C.epsb.t[:], EPS), writes=[C.epsb])
        P.op("dve", lambda e: e.memset(C.rmask.t[:], 1.0), writes=[C.rmask])
        P.op("dve", lambda e: e.memset(C.rmask.t[:].rearrange("p (c i) -> p c i", i=64)[:, :, 0:1], 0.0), writes=[C.rmask])
        P.dma("sp", C.rmat.t[:], I["rmat"], C.rmat, writes=[C.rmat])
        P.dma("sp", C.masks.t[:], I["masks"], C.masks, writes=[C.masks])

        vrow = P.sb("vrow", [128, 5, 128], F32, ph)
        P.dma("sp", vrow.t[:], I["vecs"].rearrange("(a p) f -> p a f", p=128), vrow, writes=[vrow])
        pst = P.ps("pst", [128, 512], F32, ph)
        for a in range(5):
            P.op("pe", lambda e, a=a: e.transpose(pst.t[:, 0:128], vrow.t[:, a, :], C.ident.t[:]),
                 reads=[vrow, C.ident], writes=[pst])
            P.op("dve", lambda e, a=a: e.tensor_copy(out=C.VT.t[:, a * 128:(a + 1) * 128], in_=pst.t[:, 0:128]),
                 reads=[pst], writes=[C.VT])

        scT = P.sb("scT", [128, 16, 2], F32, ph)
        P.op("act", lambda e: e.activation(out=scT.t[:, :, 0], in_=C.VT.t[:, R_C:R_C + 16], func=AF.Silu),
             reads=[C.VT], writes=[scT])
        P.op("act", lambda e: e.activation(out=scT.t[:, :, 1], in_=C.VT.t[:, R_CCTX:R_CCTX + 16], func=AF.Silu),
             reads=[C.VT], writes=[scT])

        wm = [P.sb(f"wm{j}", [128, 16, 512], F32, ph) for j in range(2)]
        psm = [P.ps(f"psm{j}", [128, 512], F32, ph) for j in range(2)]
        pstm = [P.ps(f"pstm{j}", [128, 8], F32, ph) for j in range(2)]
        mrow = [P.sb(f"mrow{j}", [2, 512], F32, ph) for j in range(2)]
        it = 0
        for l in range(2):
            wv = I["mod_w"][l].rearrange("(kc p) n -> p kc n", p=128)
            for ng in range(36):
                wb, pm, mr, pT = wm[it % 2], psm[it % 2], mrow[it % 2], pstm[it % 2]
                P.dma("sp", wb.t[:], wv[:, :, ng * 512:(ng + 1) * 512], wb, writes=[wb])
                for kc in range(16):
                    P.op("pe", lambda e, wb=wb, pm=pm, kc=kc: e.matmul(
                        pm.t[0:2, :], lhsT=scT.t[:, kc, :], rhs=wb.t[:, kc, :],
                        start=(kc == 0), stop=(kc == 15)), reads=[wb, scT], writes=[pm], inc=(kc == 15))
                P.op("act", lambda e, pm=pm, mr=mr: e.copy(out=mr.t[:], in_=pm.t[0:2, :]), reads=[pm], writes=[mr])
                for j in range(4):
                    P.op("pe", lambda e, mr=mr, pT=pT, j=j: e.transpose(
                        pT.t[:, j * 2:(j + 1) * 2], mr.t[0:2, j * 128:(j + 1) * 128], C.ident.t[0:2, 0:2]),
                        reads=[mr, C.ident], writes=[pT], inc=(j == 3))
                for j in range(4):
                    ch = ng * 4 + j
                    P.op("dve", lambda e, pT=pT, l=l, ch=ch, j=j: e.tensor_scalar(
                        out=C.MT.t[:, l, ch, :], in0=pT.t[:, j * 2:(j + 1) * 2],
                        scalar1=C.VT.t[:, R_MB + l * 144 + ch:R_MB + l * 144 + ch + 1],
                        scalar2=None, op0=ALU.add), reads=[pT, C.VT], writes=[C.MT])
                it += 1
        for l in range(2):
            for i in range(3):
                for s in range(2):
                    c0 = 16 * (3 * i)
                    P.op("dve", lambda e, l=l, i=i, s=s, c0=c0: e.scalar_tensor_tensor(
                        out=C.GS.t[:, l, i, s, :], in0=C.MT.t[:, l, c0 + 16:c0 + 32, s], scalar=1.0,
                        in1=C.VT.t[:, R_NG + (l * 3 + i) * 16:R_NG + (l * 3 + i) * 16 + 16],
                        op0=ALU.add, op1=ALU.mult), reads=[C.MT, C.VT], writes=[C.GS])
                    P.op("dve", lambda e, l=l, i=i, s=s, c0=c0: e.tensor_scalar(
                        out=C.HG.t[:, l, i, s, :], in0=C.MT.t[:, l, c0 + 32:c0 + 48, s],
                        scalar1=(1.0 if i == 1 else 0.5), scalar2=None, op0=ALU.mult), reads=[C.MT], writes=[C.HG])

        sm = C.small
        ex = P.sb("lbex", [128, 24], F32, ph)
        P.op("act", lambda e: e.activation(out=ex.t[:], in_=C.VT.t[:, R_LB:R_LB + 24], func=AF.Exp),
             reads=[C.VT], writes=[ex])
        den = P.sb("lbden", [128, 8], F32, ph)
        P.op("dve", lambda e: e.tensor_tensor(out=den.t[:], in0=ex.t[:, 0:8], in1=ex.t[:, 8:16], op=ALU.add),
             reads=[ex], writes=[den])
        P.op("dve", lambda e: e.tensor_tensor(out=den.t[:], in0=den.t[:], in1=ex.t[:, 16:24], op=ALU.add),
             reads=[ex, den], writes=[den])
        P.op("dve", lambda e: e.reciprocal(out=den.t[:], in_=den.t[:]), reads=[den], writes=[den])
        P.op("dve", lambda e: e.tensor_tensor(out=C.lb.t[:, 0:8], in0=ex.t[:, 0:8], in1=den.t[:], op=ALU.mult),
             reads=[ex, den], writes=[C.lb])
        P.op("dve", lambda e: e.tensor_scalar(out=C.lb.t[:, 8:16], in0=C.lb.t[:, 0:8], scalar1=-1.0, scalar2=1.0,
                                              op0=ALU.mult, op1=ALU.add), reads=[C.lb], writes=[C.lb])

        lr = P.sb("lamrep", [128, 512], F32, ph)
        P.dma("sp", lr.t[:], I["lamrep"], lr, writes=[lr])
        pr = P.sb("lampr", [128, 256], F32, ph)
        P.op("dve", lambda e: e.tensor_tensor(out=pr.t[:, 0:128], in0=lr.t[:, 0:128], in1=lr.t[:, 128:256], op=ALU.mult),
             reads=[lr], writes=[pr])
        P.op("dve", lambda e: e.tensor_tensor(out=pr.t[:, 128:256], in0=lr.t[:, 256:384], in1=lr.t[:, 384:512], op=ALU.mult),
             reads=[lr], writes=[pr])
        P.op("dve", lambda e: e.reduce_sum(out=sm.t[:, 1:2], in_=pr.t[:, 0:128], axis=AX.X), reads=[pr], writes=[sm])
        P.op("dve", lambda e: e.reduce_sum(out=sm.t[:, 2:3], in_=pr.t[:, 128:256], axis=AX.X), reads=[pr], writes=[sm])
        P.op("act", lambda e: e.activation(out=sm.t[:, 1:3], in_=sm.t[:, 1:3], func=AF.Exp), reads=[sm], writes=[sm])
        lam_init = 0.8 - 0.6 * math.exp(-0.3 * 0)
        P.op("dve", lambda e: e.tensor_tensor(out=sm.t[:, 0:1], in0=sm.t[:, 2:3], in1=sm.t[:, 1:2], op=ALU.subtract),
             reads=[sm], writes=[sm])
        P.op("dve", lambda e: e.tensor_scalar(out=sm.t[:, 0:1], in0=sm.t[:, 0:1], scalar1=-lam_init, scalar2=None,
                                              op0=ALU.add), reads=[sm], writes=[sm])
        C.lam_init = lam_init
        if "MTD" in C.S:
            P.dma("sp", C.S["MTD"][:, 0:576], C.MT.t[:].rearrange("p l c s -> p (l c s)"), C.MT, reads=[C.MT])
            P.dma("sp", C.S["MTD"][:, 576:768], C.GS.t[:].rearrange("p l i s c -> p (l i s c)"), C.GS, reads=[C.GS])
            P.dma("sp", C.S["MTD"][:, 768:960], C.HG.t[:].rearrange("p l i s c -> p (l i s c)"), C.HG, reads=[C.HG])

        dr = P.sb("decrep", [128, 8], F32, ph)
        P.dma("sp", dr.t[:], I["decrep"], dr, writes=[dr])
        P.op("act", lambda e: e.activation(out=dr.t[:], in_=dr.t[:], func=AF.Exp, scale=-1.0), reads=[dr], writes=[dr])
        P.op("act", lambda e: e.activation(out=sm.t[:, 16:24], in_=dr.t[:], func=AF.Ln, bias=C.ones.t[:, 0:1], scale=1.0),
             reads=[dr, C.ones], writes=[sm])
        P.op("dve", lambda e: e.tensor_scalar(out=sm.t[:, 8:16], in0=sm.t[:, 16:24], scalar1=-1.0, scalar2=None,
                                              op0=ALU.mult), reads=[sm], writes=[sm])
        P.end_phase()


def precast_job_list(C, parts):
    I, S = C.I, C.S
    jobs = []

    def ffn(l, i):
        w13v = I["ffn_w13"][l, i].rearrange("(kc p) n -> p kc n", p=128)
        w2v = I["ffn_w2"][l, i].rearrange("(fc p) d -> p fc d", p=128)
        for g in range(44):
            v4 = lambda t: t[:, 0:4096].rearrange("p (k t n) -> p k t n", k=16, t=2)
            jobs.append(([(lambda t, v4=v4: v4(t)[:, :, 0, :], w13v[:, :, g * 128:(g + 1) * 128]),
                          (lambda t, v4=v4: v4(t)[:, :, 1, :], w13v[:, :, DFF + g * 128:DFF + (g + 1) * 128])],
                         4096, S["W13B"][l, i, g]))
        for g in range(16):
            for hf in range(2):
                jobs.append(([(lambda t: t[:, 0:2816].rearrange("p (f n) -> p f n", f=22),
                               w2v[:, hf * 22:(hf + 1) * 22, g * 128:(g + 1) * 128])],
                             2816, S["W2B"][l, i, g][:, hf * 2816:(hf + 1) * 2816]))

    def win(wname, sname, ng):
        wv = I[wname].rearrange("(kc p) n -> p kc n", p=128)
        for g in range(ng):
            for hf in range(2):
                jobs.append(([(lambda t: t[:, 0:4096].rearrange("p (k n) -> p k n", k=8),
                               wv[:, hf * 8:(hf + 1) * 8, g * 512:(g + 1) * 512])],
                             4096, S[sname][g][:, hf * 4096:(hf + 1) * 4096]))

    for p_ in parts:
        if p_[0] == "ffn":
            ffn(p_[1], p_[2])
        else:
            win(p_[1], p_[2], p_[3])
    return jobs


def precast_alloc(C, ph, nb):
    P = C.P
    return {"stg": [P.sb(f"pstg{j}", [128, 4096], F32, ph) for j in range(nb)],
            "out": [P.sb(f"pout{j}", [128, 4096], BF16, ph) for j in range(nb)]}


def precast_run(C, jobs, PB, n, qld="sp", qst="pool", engs=("dve", "act")):
    P = C.P
    for _ in range(n):
        if not jobs:
            return
        loads, nel, dst = jobs.pop(0)
        k = C.pc_n
        C.pc_n += 1
        sb_, ob_ = PB["stg"][k % len(PB["stg"])], PB["out"][k % len(PB["out"])]
        eng = engs[k % len(engs)]
        for fn, ap in loads:
            P.dma(qld, fn(sb_.t), ap, sb_, writes=[sb_])
        if eng == "dve":
            P.op("dve", lambda e, sb_=sb_, ob_=ob_, nel=nel: e.tensor_copy(out=ob_.t[:, 0:nel], in_=sb_.t[:, 0:nel]),
                 reads=[sb_], writes=[ob_])
        else:
            P.op("act", lambda e, sb_=sb_, ob_=ob_, nel=nel: e.copy(out=ob_.t[:, 0:nel], in_=sb_.t[:, 0:nel]),
                 reads=[sb_], writes=[ob_])
        P.dma(qst, dst, ob_.t[:, 0:nel], ob_, reads=[ob_])


def precast_phase(C):
    P = C.P
    C.pc_n = 0
    C.late_jobs = precast_job_list(C, [("ffn", 0, 1), ("ffn", 1, 0), ("win", "odd_w_in", "WOB", 9), ("ffn", 1, 1)])
    with contextlib.ExitStack() as ph:
        PB = precast_alloc(C, ph, 4)
        jobs = precast_job_list(C, [("ffn", 0, 0), ("win", "even_w_in", "WEB", 16)])
        precast_run(C, jobs, PB, len(jobs))
        P.end_phase()


def precast_flush_phase(C):
    P = C.P
    if not C.late_jobs:
        return
    with contextlib.ExitStack() as ph:
        PB = precast_alloc(C, ph, 4)
        precast_run(C, C.late_jobs, PB, len(C.late_jobs))
        P.end_phase()


def input_phase(C):
    P, I, S = C.P, C.I, C.S
    with contextlib.ExitStack() as ph:
        xin = [P.sb(f"xin{j}", [128, 4, D], F32, ph) for j in range(2)]
        xo = [P.sb(f"xo{j}", [128, KC, 512], F32, ph) for j in range(2)]
        pt = [P.ps(f"pt{j}", [128, 512], F32, ph) for j in range(4)]
        XTv = S["XT"].rearrange("(c p) t -> p c t", p=128)
        n = 0
        for ti, (t0, tn, s) in enumerate(token_tiles(True)):
            xb, ob = xin[ti % 2], xo[ti % 2]
            nb = tn // 128
            src = I["ctx"] if s == 1 else I["x"][t0 - NCTX:t0 - NCTX + tn, :]
            P.dma("sp", xb.t[:, 0:nb, :], src.rearrange("(a p) d -> p a d", p=128), xb, writes=[xb])
            for c in range(KC):
                pb = pt[n % 4]
                n += 1
                for a in range(nb):
                    P.op("pe", lambda e, pb=pb, xb=xb, a=a, c=c: e.transpose(
                        pb.t[:, a * 128:(a + 1) * 128], xb.t[:, a, c * 128:(c + 1) * 128], C.ident.t[:]),
                        reads=[xb, C.ident], writes=[pb], inc=(a == nb - 1))
                eng = "dve" if c % 2 == 0 else "act"
                if eng == "dve":
                    P.op("dve", lambda e, pb=pb, ob=ob, c=c, tn=tn: e.tensor_copy(out=ob.t[:, c, 0:tn], in_=pb.t[:, 0:tn]),
                         reads=[pb], writes=[ob])
                else:
                    P.op("act", lambda e, pb=pb, ob=ob, c=c, tn=tn: e.copy(out=ob.t[:, c, 0:tn], in_=pb.t[:, 0:tn]),
                         reads=[pb], writes=[ob])
            P.dma("pool", XTv[:, :, t0:t0 + tn], ob.t[:, :, 0:tn], ob, reads=[ob])
        P.end_phase()


def modnorm(C, ph_bufs, xT, hT, tn, l, i, s):
    P = C.P
    sq, pss, rstd, tmp = ph_bufs
    P.op("act", lambda e: e.activation(out=rstd.t[:, 0:tn], in_=xT.t[:, 0, 0:tn], func=AF.Square), reads=[xT], writes=[rstd])
    for kc in range(1, KC):
        sb_ = sq[kc % 2]
        P.op("act", lambda e, sb_=sb_, kc=kc: e.activation(out=sb_.t[:, 0:tn], in_=xT.t[:, kc, 0:tn], func=AF.Square),
             reads=[xT], writes=[sb_])
        P.op("dve", lambda e, sb_=sb_: e.tensor_tensor(out=rstd.t[:, 0:tn], in0=rstd.t[:, 0:tn], in1=sb_.t[:, 0:tn], op=ALU.add),
             reads=[rstd, sb_], writes=[rstd])
    P.op("pe", lambda e: e.matmul(pss.t[:, 0:tn], lhsT=C.ones.t[:], rhs=rstd.t[:, 0:tn], start=True, stop=True),
         reads=[rstd, C.ones], writes=[pss])
    P.op("act", lambda e: e.activation(out=rstd.t[:, 0:tn], in_=pss.t[:, 0:tn], func=AF.Ln, bias=C.epsb.t[:, 0:1], scale=1.0 / D),
         reads=[pss, C.epsb], writes=[rstd])
    P.op("act", lambda e: e.activation(out=rstd.t[:, 0:tn], in_=rstd.t[:, 0:tn], func=AF.Exp, scale=-0.5),
         reads=[rstd], writes=[rstd])
    c0 = 16 * (3 * i)
    for kc in range(KC):
        tb = tmp[kc % 2]
        P.op("dve", lambda e, tb=tb, kc=kc: e.scalar_tensor_tensor(
            out=tb.t[:, 0:tn], in0=xT.t[:, kc, 0:tn], scalar=C.GS.t[:, l, i, s, kc:kc + 1], in1=rstd.t[:, 0:tn],
            op0=ALU.mult, op1=ALU.mult), reads=[xT, C.GS, rstd], writes=[tb])
        P.op("act", lambda e, tb=tb, kc=kc: e.activation(
            out=hT.t[:, kc, 0:tn], in_=tb.t[:, 0:tn], func=AF.Identity, bias=C.MT.t[:, l, c0 + kc, s:s + 1], scale=1.0),
            reads=[tb, C.MT], writes=[hT])


def ffn_phase(C, l, i_ffn, tiles):
    P, I, S = C.P, C.I, C.S
    isub = 0 if i_ffn == 0 else 2
    with contextlib.ExitStack() as ph:
        xTs = [P.sb(f"xT{j}", [128, KC, 512], F32, ph) for j in range(2)]
        hTs = [P.sb(f"hT{j}", [128, KC, 512], BF16, ph) for j in range(1)]
        gT = P.sb("gT", [128, FC, 512], BF16, ph)
        w13b = [P.sb(f"w13b{j}", [128, KC, 2, 128], BF16, ph) for j in range(3)]
        w2b = [P.sb(f"w2b{j}", [128, FC, 128], BF16, ph) for j in range(3)]
        sq = [P.sb(f"sq{j}", [128, 512], F32, ph) for j in range(2)]
        tmp = [P.sb(f"tmp{j}", [128, 512], F32, ph) for j in range(2)]
        rstd = P.sb("rstd", [128, 512], F32, ph)
        sa = [P.sb(f"sa{j}", [128, 512], F32, ph) for j in range(1)]
        pss = P.ps("pss", [128, 512], F32, ph)
        psa = [P.ps(f"psa{j}", [128, 512], F32, ph) for j in range(2)]
        psb = [P.ps(f"psb{j}", [128, 512], F32, ph) for j in range(2)]
        psy = [P.ps(f"psy{j}", [128, 512], F32, ph) for j in range(2)]
        XTv = S["XT"].rearrange("(c p) t -> p c t", p=128)
        cnt = {"gi": 0, "g2": 0}

        def prep(ti, t0, tn, s):
            xT, hT = xTs[ti % 2], hTs[0]
            P.dma("sp", xT.t[:, :, 0:tn], XTv[:, :, t0:t0 + tn], xT, writes=[xT])
            modnorm(C, (sq, pss, rstd, tmp), xT, hT, tn, l, isub, s)

        def tile_body(ti, t0, tn, s, mid):
            xT, hT = xTs[ti % 2], hTs[0]
            for fc in range(FC):
                wb = w13b[cnt["gi"] % 3]
                cnt["gi"] += 1
                P.dma("pool", wb.t[:], S["W13B"][l, i_ffn, fc].rearrange("p (k t n) -> p k t n", k=16, t=2), wb, writes=[wb])
                pa, pb = psa[fc % 2], psb[fc % 2]
                for kc in range(KC):
                    P.op("pe", lambda e, wb=wb, pa=pa, kc=kc: e.matmul(
                        pa.t[:, 0:tn], lhsT=wb.t[:, kc, 0, :], rhs=hT.t[:, kc, 0:tn],
                        start=(kc == 0), stop=(kc == KC - 1)), reads=[wb, hT], writes=[pa], inc=(kc == KC - 1))
                for kc in range(KC):
                    P.op("pe", lambda e, wb=wb, pb=pb, kc=kc: e.matmul(
                        pb.t[:, 0:tn], lhsT=wb.t[:, kc, 1, :], rhs=hT.t[:, kc, 0:tn],
                        start=(kc == 0), stop=(kc == KC - 1)), reads=[wb, hT], writes=[pb], inc=(kc == KC - 1))
                sb_ = sa[0]
                P.op("act", lambda e, sb_=sb_, pa=pa: e.activation(out=sb_.t[:, 0:tn], in_=pa.t[:, 0:tn], func=AF.Silu),
                     reads=[pa], writes=[sb_])
                P.op("dve", lambda e, sb_=sb_, pb=pb, fc=fc: e.tensor_tensor(
                    out=gT.t[:, fc, 0:tn], in0=sb_.t[:, 0:tn], in1=pb.t[:, 0:tn], op=ALU.mult),
                    reads=[sb_, pb], writes=[gT])
            mid()
            for dc in range(KC):
                wb = w2b[cnt["g2"] % 3]
                cnt["g2"] += 1
                P.dma("pool", wb.t[:], S["W2B"][l, i_ffn, dc].rearrange("p (f n) -> p f n", f=FC), wb, writes=[wb])
                py = psy[dc % 2]
                for fc in range(FC):
                    P.op("pe", lambda e, wb=wb, py=py, fc=fc: e.matmul(
                        py.t[:, 0:tn], lhsT=wb.t[:, fc, :], rhs=gT.t[:, fc, 0:tn],
                        start=(fc == 0), stop=(fc == FC - 1)), reads=[wb, gT], writes=[py], inc=(fc == FC - 1))
                P.op("dve", lambda e, py=py, dc=dc: e.scalar_tensor_tensor(
                    out=xT.t[:, dc, 0:tn], in0=py.t[:, 0:tn], scalar=C.HG.t[:, l, isub, s, dc:dc + 1],
                    in1=xT.t[:, dc, 0:tn], op0=ALU.mult, op1=ALU.add), reads=[py, C.HG, xT], writes=[xT])
            P.dma("sp", XTv[:, :, t0:t0 + tn], xT.t[:, :, 0:tn], xT, reads=[xT])

        prep(0, *tiles[0])
        for ti, (t0, tn, s) in enumerate(tiles):
            if ti + 1 < len(tiles):
                mid = lambda ti=ti: prep(ti + 1, *tiles[ti + 1])
            else:
                mid = lambda: None
            tile_body(ti, t0, tn, s, mid)
        P.end_phase()


def final_phase(C):
    P, I, S = C.P, C.I, C.S
    with contextlib.ExitStack() as ph:
        xT = [P.sb(f"fxT{j}", [128, KC, 512], F32, ph) for j in range(2)]
        ob = [P.sb(f"fob{j}", [128, 4, D], F32, ph) for j in range(2)]
        sq = [P.sb(f"fsq{j}", [128, 512], F32, ph) for j in range(2)]
        rstd = P.sb("frstd", [128, 512], F32, ph)
        pss = P.ps("fpss", [128, 512], F32, ph)
        pt = [P.ps(f"fpt{j}", [128, 512], F32, ph) for j in range(4)]
        XTv = S["XT"].rearrange("(c p) t -> p c t", p=128)
        n = 0
        for ti, (t0, tn, s) in enumerate(token_tiles(False)):
            xb, o = xT[ti % 2], ob[ti % 2]
            P.dma("sp", xb.t[:], XTv[:, :, t0:t0 + tn], xb, writes=[xb])
            for kc in range(KC):
                sb_ = sq[kc % 2]
                P.op("act", lambda e, sb_=sb_, kc=kc, xb=xb: e.activation(out=sb_.t[:], in_=xb.t[:, kc, :], func=AF.Square),
                     reads=[xb], writes=[sb_])
                P.op("pe", lambda e, sb_=sb_, kc=kc: e.matmul(pss.t[:], lhsT=C.ones.t[:], rhs=sb_.t[:],
                                                             start=(kc == 0), stop=(kc == KC - 1)),
                     reads=[sb_, C.ones], writes=[pss])
            P.op("act", lambda e: e.activation(out=rstd.t[:], in_=pss.t[:], func=AF.Ln, bias=C.epsb.t[:, 0:1], scale=1.0 / D),
                 reads=[pss, C.epsb], writes=[rstd])
            P.op("act", lambda e: e.activation(out=rstd.t[:], in_=rstd.t[:], func=AF.Exp, scale=-0.5),
                 reads=[rstd], writes=[rstd])
            for kc in range(KC):
                P.op("dve", lambda e, kc=kc, xb=xb: e.scalar_tensor_tensor(
                    out=xb.t[:, kc, :], in0=xb.t[:, kc, :], scalar=C.VT.t[:, R_FG + kc:R_FG + kc + 1], in1=rstd.t[:],
                    op0=ALU.mult, op1=ALU.mult), reads=[xb, C.VT, rstd], writes=[xb])
            for a in range(4):
                for cg in range(4):
                    pb = pt[n % 4]
                    n += 1
                    for cc in range(4):
                        c = cg * 4 + cc
                        P.op("pe", lambda e, pb=pb, xb=xb, a=a, c=c, cc=cc: e.transpose(
                            pb.t[:, cc * 128:(cc + 1) * 128], xb.t[:, c, a * 128:(a + 1) * 128], C.ident.t[:]),
                            reads=[xb, C.ident], writes=[pb], inc=(cc == 3))
                    if n % 2 == 0:
                        P.op("dve", lambda e, pb=pb, o=o, a=a, cg=cg: e.tensor_copy(
                            out=o.t[:, a, cg * 512:(cg + 1) * 512], in_=pb.t[:]), reads=[pb], writes=[o])
                    else:
                        P.op("act", lambda e, pb=pb, o=o, a=a, cg=cg: e.copy(
                            out=o.t[:, a, cg * 512:(cg + 1) * 512], in_=pb.t[:]), reads=[pb], writes=[o])
            r0 = t0 - NCTX
            P.dma("pool", C.out[r0:r0 + tn, :].rearrange("(a p) d -> p a d", p=128), o.t[:], o, reads=[o])
        P.end_phase()


def proj_chunks(C, ph, hT, tn, wv, chunk_list, wbufs, psp, cb, state):
    P = C.P
    groups = {}
    for j in chunk_list:
        groups.setdefault(j // 4, []).append(j)
    for gidx in sorted(groups):
        wb = wbufs[state["w"] % 2]
        state["w"] += 1
        P.dma("pool", wb.t[:], wv[gidx].rearrange("p (k n) -> p k n", k=16), wb, writes=[wb])
        for j in groups[gidx]:
            u = j % 4
            pb = psp[state["p"] % 2]
            state["p"] += 1
            for kc in range(KC):
                P.op("pe", lambda e, wb=wb, pb=pb, kc=kc, u=u: e.matmul(
                    pb.t[:, 0:tn], lhsT=wb.t[:, kc, u * 128:(u + 1) * 128], rhs=hT.t[:, kc, 0:tn],
                    start=(kc == 0), stop=(kc == KC - 1)), reads=[wb, hT], writes=[pb], inc=(kc == KC - 1))
            cb(j, pb)


class Rot:
    def __init__(self, bufs):
        self.b = bufs
        self.i = 0

    def get(self):
        b = self.b[self.i % len(self.b)]
        self.i += 1
        return b


def rope_op(C, R, src, tn, scale, dst_ap, dst_buf, Ct, St):
    P = C.P
    pr = R["psr"].get()
    P.op("pe", lambda e: e.matmul(pr.t[:, 0:tn], lhsT=C.rmat.t[:], rhs=src.t[:, 0:tn], start=True, stop=True),
         reads=[C.rmat, src], writes=[pr])
    t1 = R["f32"].get()
    t2 = R["f32"].get()
    P.op("dve", lambda e: e.scalar_tensor_tensor(out=t1.t[:, 0:tn], in0=src.t[:, 0:tn], scalar=float(scale),
                                                 in1=Ct.t[:, 0:tn], op0=ALU.mult, op1=ALU.mult),
         reads=[src, Ct], writes=[t1])
    P.op("dve", lambda e: e.scalar_tensor_tensor(out=t2.t[:, 0:tn], in0=pr.t[:, 0:tn], scalar=float(scale),
                                                 in1=St.t[:, 0:tn], op0=ALU.mult, op1=ALU.mult),
         reads=[pr, St], writes=[t2])
    P.op("dve", lambda e: e.tensor_tensor(out=dst_ap, in0=t1.t[:, 0:tn], in1=t2.t[:, 0:tn], op=ALU.add),
         reads=[t1, t2], writes=[dst_buf])


def store_tokmajor(C, R, src, tn, dram_rows):
    P = C.P
    nb = tn // 128
    pt = R["pstb"].get()
    for a in range(nb):
        P.op("pe", lambda e, a=a: e.transpose(pt.t[:, a * 128:(a + 1) * 128], src.t[:, a * 128:(a + 1) * 128], C.identb.t[:]),
             reads=[src, C.identb], writes=[pt], inc=(a == nb - 1))
    st = R["tokb"].get()
    P.op("act", lambda e: e.copy(out=st.t[:, 0:nb * 128], in_=pt.t[:, 0:nb * 128]), reads=[pt], writes=[st])
    P.dma("sp", dram_rows.rearrange("(a p) f -> p a f", p=128),
          st.t[:, 0:nb * 128].rearrange("p (a f) -> p a f", f=128), st, reads=[st])


def mk_rot(C, ph, tagp=""):
    P = C.P
    R = {}
    R["f32"] = Rot([P.sb(f"rf{j}", [128, 512], F32, ph) for j in range(6)])
    R["b16"] = Rot([P.sb(f"rb{j}", [128, 512], BF16, ph) for j in range(4)])
    R["tokb"] = Rot([P.sb(f"tk{j}", [128, 512], BF16, ph) for j in range(2)])
    R["psr"] = Rot([P.ps(f"psr{j}", [128, 512], F32, ph) for j in range(1)])
    R["pstb"] = Rot([P.ps(f"pstb{j}", [128, 512], BF16, ph) for j in range(2)])
    return R


def hgrn_gate(C, R, G, pb, tn, dirn, h, QH, t0):
    P, S = C.P, C.S
    nch = tn // 64
    dh = dirn * 8 + h
    A, An, Gl, Tm, BC, E = G["A"], G["F"], G["G"], G["K"], G["BC"], G["E"]
    oml = C.lb.t[:, 8 + h:9 + h]
    lbh = C.lb.t[:, h:h + 1]
    v3 = lambda b_: b_.t[:, 0:tn].rearrange("p (c i) -> p c i", i=64)
    P.op("act", lambda e: e.activation(out=A.t[:, 0:tn], in_=pb.t[:, 0:tn], func=AF.Sigmoid), reads=[pb], writes=[A])
    P.op("act", lambda e: e.activation(out=An.t[:, 0:tn], in_=pb.t[:, 0:tn], func=AF.Sigmoid, scale=-1.0), reads=[pb], writes=[An])
    P.op("act", lambda e: e.activation(out=Gl.t[:, 0:tn], in_=A.t[:, 0:tn], func=AF.Ln, scale=oml, bias=lbh),
         reads=[A, C.lb], writes=[Gl])
    yield
    P.op("dve", lambda e: e.tensor_tensor_scan(out=BC.t[:, 0:tn], data0=C.rmask.t[:, 0:tn], data1=Gl.t[:, 0:tn],
                                               initial=0.0, op0=ALU.mult, op1=ALU.add), reads=[Gl, C.rmask], writes=[BC])
    if dirn == 0:
        Eb = BC
    else:
        Eb = E
        P.op("dve", lambda e: e.tensor_tensor(out=E.t[:, 0:tn], in0=Gl.t[:, 0:tn], in1=BC.t[:, 0:tn], op=ALU.subtract),
             reads=[Gl, BC], writes=[E])
        P.op("dve", lambda e: e.tensor_tensor(out=v3(E), in0=v3(E), in1=v3(BC)[:, :, 63:64].broadcast_to([128, nch, 64]),
                                              op=ALU.add), reads=[E, BC], writes=[E])
    yield
    dl = G["dl"].get()
    P.op("act", lambda e: e.activation(out=dl.t[:, 0:nch, 0], in_=v3(BC)[:, :, 63], func=AF.Exp), reads=[BC], writes=[dl])
    P.dma("sp", S["GDL"][dh][:, t0 // 64:t0 // 64 + nch], dl.t[:, 0:nch, 0], dl, reads=[dl])
    X1 = G["X1"]
    P.op("act", lambda e: e.activation(out=X1.t[:, 0:tn], in_=Eb.t[:, 0:tn], func=AF.Exp), reads=[Eb], writes=[X1])
    X2 = G["X2"]
    P.op("act", lambda e: e.activation(out=X2.t[:, 0:tn], in_=Eb.t[:, 0:tn], func=AF.Exp, scale=-1.0), reads=[Eb], writes=[X2])
    yield
    o1 = R["b16"].get()
    P.op("dve", lambda e: e.tensor_tensor(out=o1.t[:, 0:tn], in0=QH.t[:, h, 0:tn], in1=X1.t[:, 0:tn], op=ALU.mult),
         reads=[QH, X1], writes=[o1])
    P.dma("sp", S["GQ"][dh][:, t0:t0 + tn], o1.t[:, 0:tn], o1, reads=[o1])
    P.op("dve", lambda e: e.scalar_tensor_tensor(out=Tm.t[:, 0:tn], in0=An.t[:, 0:tn], scalar=oml, in1=X2.t[:, 0:tn],
                                                 op0=ALU.mult, op1=ALU.mult), reads=[An, C.lb, X2], writes=[Tm])
    o2 = R["b16"].get()
    P.op("act", lambda e: e.copy(out=o2.t[:, 0:tn], in_=Tm.t[:, 0:tn]), reads=[Tm], writes=[o2])
    P.dma("sp", S["GK"][dh][:, t0:t0 + tn], o2.t[:, 0:tn], o2, reads=[o2])
    o3 = G["o3"]
    P.op("dve", lambda e: e.tensor_tensor(out=o3.t[:, 0:tn].rearrange("p (c i) -> p c i", i=64), in0=v3(Tm),
                                          in1=dl.t[:, 0:nch, :].broadcast_to([128, nch, 64]), op=ALU.mult),
         reads=[Tm, dl], writes=[o3])
    yield
    store_tokmajor(C, R, o3, tn, S["GKH"][dh][t0:t0 + tn, :])


class Pipe:
    def __init__(self):
        self.fl = []

    def advance(self):
        for g in list(self.fl):
            try:
                next(g)
            except StopIteration:
                self.fl.remove(g)

    def start(self, g):
        try:
            next(g)
            self.fl.append(g)
        except StopIteration:
            pass

    def drain(self):
        while self.fl:
            self.advance()


def mixer_proj_common(C, ph, l):
    P = C.P
    B = {}
    B["xT"] = P.sb("xT", [128, KC, 512], F32, ph)
    B["hT"] = P.sb("hT", [128, KC, 512], BF16, ph)
    B["wb"] = [P.sb(f"wb{j}", [128, KC, 512], BF16, ph) for j in range(2)]
    B["sq"] = [P.sb(f"sq{j}", [128, 512], F32, ph) for j in range(2)]
    B["tmp"] = [P.sb(f"tmp{j}", [128, 512], F32, ph) for j in range(2)]
    B["rstd"] = P.sb("rstd", [128, 512], F32, ph)
    B["pss"] = P.ps("pss", [128, 512], F32, ph)
    B["psp"] = [P.ps(f"psp{j}", [128, 512], F32, ph) for j in range(2)]
    B["Ct"] = P.sb("Ct", [128, 512], F32, ph)
    B["St"] = P.sb("St", [128, 512], F32, ph)
    return B


def even_proj_phase(C):
    P, I, S = C.P, C.I, C.S
    with contextlib.ExitStack() as ph:
        B = mixer_proj_common(C, ph, 0)
        R = mk_rot(C, ph)
        dlrot = Rot([P.sb(f"gdl{j}", [128, 8, 1], F32, ph) for j in range(4)])
        gA = P.sb("gA", [128, 512], F32, ph)
        gTm = P.sb("gTm", [128, 512], F32, ph)
        Gs = []
        for gj in range(3):
            Gd = {k: P.sb(f"g{k}{gj}", [128, 512], F32, ph) for k in ("F", "G", "BC", "E", "X1", "X2")}
            Gd["o3"] = P.sb(f"go3{gj}", [128, 512], BF16, ph)
            Gd["A"] = gA
            Gd["K"] = gTm
            Gd["dl"] = dlrot
            Gs.append(Gd)
        Grot = Rot(Gs)
        HF = Rot([P.sb(f"hf{j}", [128, 512], F32, ph) for j in range(3)])
        HB = Rot([P.sb(f"hb{j}", [128, 512], BF16, ph) for j in range(3)])
        pipe = Pipe()
        QH = P.sb("QH", [128, 8, 512], F32, ph)
        xT, hT, Ct, St = B["xT"], B["hT"], B["Ct"], B["St"]
        XTv = S["XT"].rearrange("(c p) t -> p c t", p=128)
        wv = S["WEB"]
        state = {"w": 0, "p": 0}
        qs = 128.0 ** -0.5
        for (t0, tn, s) in token_tiles(True):
            P.dma("sp", xT.t[:, :, 0:tn], XTv[:, :, t0:t0 + tn], xT, writes=[xT])
            P.dma("sp", Ct.t[:, 0:tn], I["ropec"][:, t0:t0 + tn], Ct, writes=[Ct])
            P.dma("sp", St.t[:, 0:tn], I["ropes"][:, t0:t0 + tn], St, writes=[St])
            modnorm(C, (B["sq"], B["pss"], B["rstd"], B["tmp"]), xT, hT, tn, 0, 1, s)

            def item(j, pb, t0=t0, tn=tn):
                if j < 8:
                    tq = HF.get()
                    P.op("act", lambda e: e.activation(out=tq.t[:, 0:tn], in_=pb.t[:, 0:tn], func=AF.Silu),
                         reads=[pb], writes=[tq])
                    yield
                    P.op("dve", lambda e: e.tensor_scalar(out=QH.t[:, j, 0:tn], in0=tq.t[:, 0:tn], scalar1=qs,
                                                          scalar2=None, op0=ALU.mult), reads=[tq], writes=[QH])
                elif j < 24:
                    yield from hgrn_gate(C, R, Grot.get(), pb, tn, (j - 8) // 8, (j - 8) % 8, QH, t0)
                elif j < 32 or j >= 56:
                    vb = HB.get()
                    P.op("act", lambda e: e.copy(out=vb.t[:, 0:tn], in_=pb.t[:, 0:tn]), reads=[pb], writes=[vb])
                    if j < 32:
                        dst = S["GV"][t0:t0 + tn, (j - 24) * 128:(j - 23) * 128]
                    else:
                        dst = S["AV"][t0:t0 + tn, (j - 56) * 128:(j - 55) * 128]
                    yield
                    store_tokmajor(C, R, vb, tn, dst)
                elif j < 40:
                    gt = R["f32"].get()
                    P.op("act", lambda e: e.activation(out=gt.t[:, 0:tn], in_=pb.t[:, 0:tn], func=AF.Sigmoid),
                         reads=[pb], writes=[gt])
                    P.dma("sp", S["GG"][(j - 32) * 128:(j - 31) * 128, t0:t0 + tn], gt.t[:, 0:tn], gt, reads=[gt])
                else:
                    isq = j < 48
                    qf = HF.get()
                    P.op("act", lambda e: e.copy(out=qf.t[:, 0:tn], in_=pb.t[:, 0:tn]), reads=[pb], writes=[qf])
                    yield
                    ob = R["b16"].get()
                    rope_op(C, R, qf, tn, qs if isq else 1.0, ob.t[:, 0:tn], ob, Ct, St)
                    dst = S["AQ"] if isq else S["AK"]
                    r0 = (j - 40) * 128 if isq else (j - 48) * 128
                    P.dma("sp", dst[r0:r0 + 128, t0:t0 + tn], ob.t[:, 0:tn], ob, reads=[ob])

            def cb(j, pb):
                pipe.advance()
                pipe.start(item(j, pb))

            proj_chunks(C, ph, hT, tn, wv, list(range(64)), B["wb"], B["psp"], cb, state)
            pipe.drain()
        P.end_phase()


def gla_scan_phase(C, nH, dv, dirn, qdir_of, late=0):
    P, S = C.P, C.S
    nvc = dv // 128
    with contextlib.ExitStack() as ph:
        St_ = [P.sb(f"S{h}", [128, dv], F32, ph) for h in range(nH)]
        Sb_ = [P.sb(f"Sb{h}", [128, dv], BF16, ph) for h in range(nH)]
        DLt = P.sb("DLt", [128, nH, NCH], F32, ph)
        LD = [[{"q": P.sb(f"lq{p}{h}", [128, 512], BF16, ph), "k": P.sb(f"lk{p}{h}", [128, 512], BF16, ph),
                "kh": P.sb(f"lkh{p}{h}", [64, 8, 128], BF16, ph), "v": P.sb(f"lv{p}{h}", [64, 8, dv], BF16, ph)}
               for h in range(nH)] for p in range(2)]
        OT = [[P.sb(f"ot{p}{h}", [128, nvc, 512], F32, ph) for h in range(nH)] for p in range(2)]
        pts = Rot([P.sb(f"pt{j}", [64, 64], BF16, ph) for j in range(4)])
        pss = Rot([P.ps(f"pss{j}", [64, 64], F32, ph) for j in range(2)])
        pso = Rot([P.ps(f"pso{j}", [128, 64], F32, ph) for j in range(3)])
        psu = Rot([P.ps(f"psu{j}", [128, dv], F32, ph) for j in range(2)])
        for h in range(nH):
            P.op("dve", lambda e, h=h: e.memset(St_[h].t[:], 0.0), writes=[St_[h]])
            P.op("dve", lambda e, h=h: e.memset(Sb_[h].t[:], 0.0), writes=[Sb_[h]])
            P.dma("sp", DLt.t[:, h, :], S["GDL"][dirn * nH + h], DLt, writes=[DLt])
        tiles = token_tiles(True)
        order = tiles if dirn == 0 else [tiles[0]] + tiles[:0:-1]
        mk = C.masks.t[:, 0:64] if dirn == 0 else C.masks.t[:, 64:128]
        PBl = precast_alloc(C, ph, 2) if late else None
        for ti, (t0, tn, s) in enumerate(order):
            nch = tn // 64
            p = ti % 2
            if late:
                precast_run(C, C.late_jobs, PBl, late, qld="pool", qst="pool", engs=("dve", "act"))
            for h in range(nH):
                L = LD[p][h]
                dq = qdir_of(dirn) * nH + h
                dk = dirn * nH + h
                P.dma("sp", L["q"].t[:, 0:tn], S["GQ"][dq][:, t0:t0 + tn], L["q"], writes=[L["q"]])
                P.dma("sp", L["k"].t[:, 0:tn], S["GK"][dk][:, t0:t0 + tn], L["k"], writes=[L["k"]])
                P.dma("sp", L["kh"].t[:, 0:nch, :], S["GKH"][dk][t0:t0 + tn, :].rearrange("(c j) k -> j c k", j=64),
                      L["kh"], writes=[L["kh"]])
                P.dma("sp", L["v"].t[:, 0:nch, :], S["GV"][t0:t0 + tn, h * dv:(h + 1) * dv].rearrange("(c j) v -> j c v", j=64),
                      L["v"], writes=[L["v"]])
            corder = range(nch) if dirn == 0 else range(nch - 1, -1, -1)
            for c in corder:
                cg = t0 // 64 + c
                for h in range(nH):
                    L = LD[p][h]
                    ps_s = pss.get()
                    P.op("pe", lambda e, L=L, c=c, ps_s=ps_s: e.matmul(
                        ps_s.t[:], lhsT=L["k"].t[:, c * 64:(c + 1) * 64], rhs=L["q"].t[:, c * 64:(c + 1) * 64],
                        start=True, stop=True), reads=[L["k"], L["q"]], writes=[ps_s])
                    pt = pts.get()
                    P.op("dve", lambda e, pt=pt, ps_s=ps_s: e.tensor_tensor(out=pt.t[:], in0=ps_s.t[:], in1=mk, op=ALU.mult),
                         reads=[ps_s, C.masks], writes=[pt])
                    for vc in range(nvc):
                        po = pso.get()
                        P.op("pe", lambda e, L=L, c=c, vc=vc, po=po, pt=pt: e.matmul(
                            po.t[:], lhsT=L["v"].t[:, c, vc * 128:(vc + 1) * 128], rhs=pt.t[:], start=True, stop=False),
                            reads=[L["v"], pt], writes=[po], inc=False)
                        P.op("pe", lambda e, L=L, c=c, vc=vc, po=po, h=h: e.matmul(
                            po.t[:], lhsT=Sb_[h].t[:, vc * 128:(vc + 1) * 128], rhs=L["q"].t[:, c * 64:(c + 1) * 64],
                            start=False, stop=True), reads=[Sb_[h], L["q"]], writes=[po])
                        P.op("act", lambda e, po=po, vc=vc, c=c, h=h, p=p: e.copy(
                            out=OT[p][h].t[:, vc, c * 64:(c + 1) * 64], in_=po.t[:]), reads=[po], writes=[OT[p][h]])
                    pu = psu.get()
                    P.op("pe", lambda e, L=L, c=c, pu=pu: e.matmul(
                        pu.t[:], lhsT=L["kh"].t[:, c, :], rhs=L["v"].t[:, c, :], start=True, stop=True),
                        reads=[L["kh"], L["v"]], writes=[pu])
                    P.op("dve", lambda e, pu=pu, h=h, cg=cg: e.scalar_tensor_tensor(
                        out=St_[h].t[:], in0=St_[h].t[:], scalar=DLt.t[:, h, cg:cg + 1], in1=pu.t[:],
                        op0=ALU.mult, op1=ALU.add), reads=[St_[h], DLt, pu], writes=[St_[h]])
                    P.op("act", lambda e, h=h: e.copy(out=Sb_[h].t[:], in_=St_[h].t[:]), reads=[St_[h]], writes=[Sb_[h]])
            for h in range(nH):
                for vc in range(nvc):
                    r0 = h * dv + vc * 128
                    P.dma("sp", S["OH"][dirn][r0:r0 + 128, t0:t0 + tn], OT[p][h].t[:, vc, 0:tn], OT[p][h], reads=[OT[p][h]])
        P.end_phase()


def headnorm_phase(C, nH, nvc, gcol, ym_row0, tiles):
    P, S = C.P, C.S
    nchunk = nH * nvc
    with contextlib.ExitStack() as ph:
        oa = [P.sb(f"oa{j}", [128, nchunk, 512], F32, ph) for j in range(2)]
        ob = [P.sb(f"ob{j}", [128, nchunk, 512], F32, ph) for j in range(2)]
        gg = [P.sb(f"gg{j}", [128, nchunk, 512], F32, ph) for j in range(2)]
        yb = [P.sb(f"yb{j}", [128, nchunk, 512], BF16, ph) for j in range(2)]
        sq = Rot([P.sb(f"sq{j}", [128, 512], F32, ph) for j in range(2)])
        rs = Rot([P.sb(f"rs{j}", [128, 512], F32, ph) for j in range(2)])
        tm = Rot([P.sb(f"tm{j}", [128, 512], F32, ph) for j in range(2)])
        pss = Rot([P.ps(f"pss{j}", [128, 512], F32, ph) for j in range(2)])
        v3 = lambda ap: ap.rearrange("(c p) t -> p c t", p=128)
        def tile_body(ti, t0, tn, s):
            a, b, g, y = oa[ti % 2], ob[ti % 2], gg[ti % 2], yb[ti % 2]
            P.dma("sp", a.t[:, :, 0:tn], v3(S["OH"][0][0:nchunk * 128, t0:t0 + tn]), a, writes=[a])
            P.dma("sp", b.t[:, :, 0:tn], v3(S["OH"][1][0:nchunk * 128, t0:t0 + tn]), b, writes=[b])
            P.dma("sp", g.t[:, :, 0:tn], v3(S["GG"][0:nchunk * 128, t0:t0 + tn]), g, writes=[g])
            for ch in range(nchunk):
                P.op("dve", lambda e, ch=ch, a=a, b=b: e.tensor_tensor(
                    out=a.t[:, ch, 0:tn], in0=a.t[:, ch, 0:tn], in1=b.t[:, ch, 0:tn], op=ALU.add), reads=[a, b], writes=[a])
            for h in range(nH):
                ps_ = pss.get()
                for vc in range(nvc):
                    ch = h * nvc + vc
                    s_ = sq.get()
                    P.op("act", lambda e, ch=ch, a=a, s_=s_: e.activation(out=s_.t[:, 0:tn], in_=a.t[:, ch, 0:tn], func=AF.Square),
                         reads=[a], writes=[s_])
                    P.op("pe", lambda e, s_=s_, ps_=ps_, vc=vc: e.matmul(
                        ps_.t[:, 0:tn], lhsT=C.ones.t[:], rhs=s_.t[:, 0:tn], start=(vc == 0), stop=(vc == nvc - 1)),
                        reads=[s_, C.ones], writes=[ps_])
                r_ = rs.get()
                P.op("act", lambda e, r_=r_, ps_=ps_: e.activation(out=r_.t[:, 0:tn], in_=ps_.t[:, 0:tn], func=AF.Ln,
                                                                 bias=C.epsb.t[:, 0:1], scale=1.0 / (128 * nvc)),
                     reads=[ps_, C.epsb], writes=[r_])
                P.op("act", lambda e, r_=r_: e.activation(out=r_.t[:, 0:tn], in_=r_.t[:, 0:tn], func=AF.Exp, scale=-0.5),
                     reads=[r_], writes=[r_])
                for vc in range(nvc):
                    ch = h * nvc + vc
                    t_ = tm.get()
                    P.op("dve", lambda e, ch=ch, vc=vc, a=a, r_=r_, t_=t_: e.scalar_tensor_tensor(
                        out=t_.t[:, 0:tn], in0=a.t[:, ch, 0:tn], scalar=C.VT.t[:, gcol + vc:gcol + vc + 1], in1=r_.t[:, 0:tn],
                        op0=ALU.mult, op1=ALU.mult), reads=[a, C.VT, r_], writes=[t_])
                    P.op("dve", lambda e, ch=ch, t_=t_, g=g, y=y: e.tensor_tensor(
                        out=y.t[:, ch, 0:tn], in0=t_.t[:, 0:tn], in1=g.t[:, ch, 0:tn], op=ALU.mult), reads=[t_, g], writes=[y])
            P.dma("sp", v3(S["YM"][ym_row0:ym_row0 + nchunk * 128, t0:t0 + tn]), y.t[:, :, 0:tn], y, reads=[y])

        for ti, (t0, tn, s) in enumerate(tiles):
            tile_body(ti, t0, tn, s)
        P.end_phase()


def attention_phase(C, units, dv, finish, extra_alloc=None, late=0):
    P = C.P
    nvc = dv // 128
    NKC = T // 128
    with contextlib.ExitStack() as ph:
        nk = max(len(u["ks"]) for u in units)
        nm = max(len(u["maps"]) for u in units)
        KT = [[P.sb(f"KT{p}{j}", [128, T], BF16, ph) for j in range(nk)] for p in range(2)]
        VV = [P.sb(f"VV{p}", [128, NKC, dv], BF16, ph) for p in range(2)]
        qb = Rot([P.sb(f"qb{j}", [128, 512], BF16, ph) for j in range(3)])
        ptb = Rot([P.sb(f"ptb{j}", [128, 512], BF16, ph) for j in range(4)])
        ob = [[P.sb(f"o{p}{m}", [128, nvc, 512], F32, ph) for m in range(nm)] for p in range(2)]
        rden = Rot([P.sb(f"rden{j}", [128, 512], F32, ph) for j in range(2)])
        pss = Rot([P.ps(f"aps{j}", [128, 512], F32, ph) for j in range(3)])
        pso = [P.ps(f"apo{j}", [128, 512], F32, ph) for j in range(nvc)]
        psd = P.ps("apd", [128, 512], F32, ph)
        X = extra_alloc(ph) if extra_alloc else None
        PBl = precast_alloc(C, ph, 2) if late else None
        qi = 0
        for ui, u in enumerate(units):
            p = ui % 2
            for j, kap in enumerate(u["ks"]):
                P.dma("sp", KT[p][j].t[:], kap, KT[p][j], writes=[KT[p][j]])
            P.dma("sp", VV[p].t[:], u["v"].rearrange("(c p) v -> p c v", p=128), VV[p], writes=[VV[p]])
            def qtile_body(u, p, t0, tn, nkc, obs):
                for m, (qap, kidx) in enumerate(u["maps"]):
                    if late:
                        precast_run(C, C.late_jobs, PBl, late, qld="pool", qst="pool", engs=("dve",))
                    q = qb.get()
                    P.dma("sp", q.t[:, 0:tn], qap[:, t0:t0 + tn], q, writes=[q])
                    K = KT[p][kidx]

                    def s_mm(kc, K=K, q=q):
                        ps_ = pss.get()
                        P.op("pe", lambda e, K=K, kc=kc, q=q, ps_=ps_: e.matmul(
                            ps_.t[:, 0:tn], lhsT=K.t[:, kc * 128:(kc + 1) * 128], rhs=q.t[:, 0:tn], start=True, stop=True),
                            reads=[K, q], writes=[ps_])
                        return ps_

                    LOOK = 2
                    pend = [s_mm(kc) for kc in range(min(LOOK, nkc))]
                    for kc in range(nkc):
                        if kc + LOOK < nkc:
                            pend.append(s_mm(kc + LOOK))
                        ps_ = pend.pop(0)
                        pt = ptb.get()
                        P.op("act", lambda e, pt=pt, ps_=ps_: e.activation(out=pt.t[:, 0:tn], in_=ps_.t[:, 0:tn], func=AF.Exp),
                             reads=[ps_], writes=[pt])
                        for vc in range(nvc):
                            P.op("pe", lambda e, vc=vc, kc=kc, pt=pt, p=p: e.matmul(
                                pso[vc].t[:, 0:tn], lhsT=VV[p].t[:, kc, vc * 128:(vc + 1) * 128], rhs=pt.t[:, 0:tn],
                                start=(kc == 0), stop=(kc == nkc - 1)), reads=[VV[p], pt], writes=[pso[vc]], inc=False)
                        P.op("pe", lambda e, kc=kc, pt=pt: e.matmul(
                            psd.t[:, 0:tn], lhsT=C.onesb.t[:], rhs=pt.t[:, 0:tn], start=(kc == 0), stop=(kc == nkc - 1)),
                            reads=[C.onesb, pt], writes=[psd])
                    rd = rden.get()
                    P.op("dve", lambda e, rd=rd: e.reciprocal(out=rd.t[:, 0:tn], in_=psd.t[:, 0:tn]), reads=[psd], writes=[rd])
                    for vc in range(nvc):
                        P.op("dve", lambda e, vc=vc, rd=rd, m=m, obs=obs: e.tensor_tensor(
                            out=obs[m].t[:, vc, 0:tn], in0=pso[vc].t[:, 0:tn], in1=rd.t[:, 0:tn], op=ALU.mult),
                            reads=[pso[vc], rd], writes=[obs[m]])
                finish(u, t0, tn, obs, X)

            for (t0, tn, nkc) in u["qtiles"]:
                obs = ob[qi % 2]
                qi += 1
                qtile_body(u, p, t0, tn, nkc, obs)
        P.end_phase()


def out_proj_phase(C, l, w_out, tiles):
    P, I, S = C.P, C.I, C.S
    with contextlib.ExitStack() as ph:
        W = P.sb("wo", [128, KC, D], BF16, ph)
        ym = [P.sb(f"ym{j}", [128, KC, 512], BF16, ph) for j in range(2)]
        xT = [P.sb(f"oxT{j}", [128, KC, 512], F32, ph) for j in range(2)]
        psy = Rot([P.ps(f"opy{j}", [128, 512], F32, ph) for j in range(3)])
        wv = w_out.rearrange("(kc p) n -> p kc n", p=128)
        for hh in range(4):
            P.dma("pool", W.t[:, :, hh * 512:(hh + 1) * 512], wv[:, :, hh * 512:(hh + 1) * 512], W, writes=[W])
        XTv = S["XT"].rearrange("(c p) t -> p c t", p=128)
        YMv = S["YM"].rearrange("(c p) t -> p c t", p=128)
        def tile_body(ti, t0, tn, s):
            y, x = ym[ti % 2], xT[ti % 2]
            P.dma("sp", y.t[:, :, 0:tn], YMv[:, :, t0:t0 + tn], y, writes=[y])
            P.dma("sp", x.t[:, :, 0:tn], XTv[:, :, t0:t0 + tn], x, writes=[x])
            for dc in range(KC):
                py = psy.get()
                for kc in range(KC):
                    P.op("pe", lambda e, py=py, kc=kc, dc=dc, y=y: e.matmul(
                        py.t[:, 0:tn], lhsT=W.t[:, kc, dc * 128:(dc + 1) * 128], rhs=y.t[:, kc, 0:tn],
                        start=(kc == 0), stop=(kc == KC - 1)), reads=[W, y], writes=[py], inc=(kc == KC - 1))
                P.op("dve", lambda e, py=py, dc=dc, x=x, s=s: e.scalar_tensor_tensor(
                    out=x.t[:, dc, 0:tn], in0=py.t[:, 0:tn], scalar=C.HG.t[:, l, 1, s, dc:dc + 1], in1=x.t[:, dc, 0:tn],
                    op0=ALU.mult, op1=ALU.add), reads=[py, C.HG, x], writes=[x])
            P.dma("sp", XTv[:, :, t0:t0 + tn], x.t[:, :, 0:tn], x, reads=[x])

        for ti, (t0, tn, s) in enumerate(tiles):
            tile_body(ti, t0, tn, s)
        P.end_phase()


def even_mixer(C):
    P, S = C.P, C.S
    even_proj_phase(C)
    if C.stop == "m0p":
        return
    gla_scan_phase(C, 8, 128, 0, lambda d: d, late=2)
    gla_scan_phase(C, 8, 128, 1, lambda d: d, late=2)
    if C.stop == "m0s":
        return
    headnorm_phase(C, 8, 1, R_HGG, 0, token_tiles(True))
    lat_q = [(NCTX + 512 * i, 512, T // 128) for i in range(8)]
    units = []
    for hh in range(4):
        units.append(dict(ks=[S["AK"][(2 * hh) * 128:(2 * hh + 1) * 128, :], S["AK"][(2 * hh + 1) * 128:(2 * hh + 2) * 128, :]],
                          maps=[(S["AQ"][(2 * hh) * 128:(2 * hh + 1) * 128, :], 0), (S["AQ"][(2 * hh + 1) * 128:(2 * hh + 2) * 128, :], 1)],
                          v=S["AV"][:, hh * 256:(hh + 1) * 256], qtiles=[(0, NCTX, NCTX // 128)] + lat_q, hh=hh))

    def alloc(ph):
        X = {}
        X["dd"] = Rot([P.sb(f"dd{j}", [128, 2, 512], F32, ph) for j in range(2)])
        X["sq"] = Rot([P.sb(f"dsq{j}", [128, 512], F32, ph) for j in range(2)])
        X["rs"] = Rot([P.sb(f"drs{j}", [128, 512], F32, ph) for j in range(2)])
        X["tm"] = Rot([P.sb(f"dtm{j}", [128, 512], F32, ph) for j in range(2)])
        X["yb"] = Rot([P.sb(f"dyb{j}", [128, 2, 512], BF16, ph) for j in range(2)])
        X["ps"] = P.ps("dps", [128, 512], F32, ph)
        return X

    def finish(u, t0, tn, obs, X):
        hh = u["hh"]
        dd = X["dd"].get()
        for vc in range(2):
            P.op("dve", lambda e, vc=vc: e.scalar_tensor_tensor(
                out=dd.t[:, vc, 0:tn], in0=obs[1].t[:, vc, 0:tn], scalar=C.small.t[:, 0:1], in1=obs[0].t[:, vc, 0:tn],
                op0=ALU.mult, op1=ALU.add), reads=[obs[0], obs[1], C.small], writes=[dd])
        ps_ = X["ps"]
        for vc in range(2):
            s_ = X["sq"].get()
            P.op("act", lambda e, vc=vc, s_=s_: e.activation(out=s_.t[:, 0:tn], in_=dd.t[:, vc, 0:tn], func=AF.Square),
                 reads=[dd], writes=[s_])
            P.op("pe", lambda e, vc=vc, s_=s_: e.matmul(ps_.t[:, 0:tn], lhsT=C.ones.t[:], rhs=s_.t[:, 0:tn],
                                                       start=(vc == 0), stop=(vc == 1)), reads=[s_, C.ones], writes=[ps_])
        r_ = X["rs"].get()
        P.op("act", lambda e: e.activation(out=r_.t[:, 0:tn], in_=ps_.t[:, 0:tn], func=AF.Ln, bias=C.epsb.t[:, 0:1], scale=1.0 / 256),
             reads=[ps_, C.epsb], writes=[r_])
        P.op("act", lambda e: e.activation(out=r_.t[:, 0:tn], in_=r_.t[:, 0:tn], func=AF.Exp, scale=-0.5), reads=[r_], writes=[r_])
        yb = X["yb"].get()
        for vc in range(2):
            t_ = X["tm"].get()
            P.op("dve", lambda e, vc=vc, t_=t_: e.scalar_tensor_tensor(
                out=t_.t[:, 0:tn], in0=dd.t[:, vc, 0:tn], scalar=C.VT.t[:, R_DAG + vc:R_DAG + vc + 1], in1=r_.t[:, 0:tn],
                op0=ALU.mult, op1=ALU.mult), reads=[dd, C.VT, r_], writes=[t_])
            P.op("dve", lambda e, vc=vc, t_=t_: e.tensor_scalar(
                out=yb.t[:, vc, 0:tn], in0=t_.t[:, 0:tn], scalar1=float(1.0 - C.lam_init), scalar2=None, op0=ALU.mult),
                reads=[t_], writes=[yb])
        r0 = 1024 + hh * 256
        P.dma("sp", S["YM"][r0:r0 + 256, t0:t0 + tn].rearrange("(c p) t -> p c t", p=128), yb.t[:, :, 0:tn], yb, reads=[yb])

    attention_phase(C, units, 256, finish, alloc, late=3)
    precast_flush_phase(C)
    if C.stop == "m0a":
        return
    out_proj_phase(C, 0, C.I["even_w_out"], token_tiles(True))


def qknorm(C, R, B, pb, tn, gcol):
    P = C.P
    qf = R["q_qf"].get()
    P.op("act", lambda e: e.copy(out=qf.t[:, 0:tn], in_=pb.t[:, 0:tn]), reads=[pb], writes=[qf])
    s_ = R["q_sq"].get()
    P.op("act", lambda e: e.activation(out=s_.t[:, 0:tn], in_=pb.t[:, 0:tn], func=AF.Square), reads=[pb], writes=[s_])
    yield
    ps_ = B["pss"]
    P.op("pe", lambda e: e.matmul(ps_.t[:, 0:tn], lhsT=C.ones.t[:], rhs=s_.t[:, 0:tn], start=True, stop=True),
         reads=[s_, C.ones], writes=[ps_])
    yield
    r_ = R["q_r"].get()
    P.op("act", lambda e: e.activation(out=r_.t[:, 0:tn], in_=ps_.t[:, 0:tn], func=AF.Ln, bias=C.epsb.t[:, 0:1], scale=1.0 / 128),
         reads=[ps_, C.epsb], writes=[r_])
    P.op("act", lambda e: e.activation(out=r_.t[:, 0:tn], in_=r_.t[:, 0:tn], func=AF.Exp, scale=-0.5), reads=[r_], writes=[r_])
    yield
    qn = R["q_qn"].get()
    P.op("dve", lambda e: e.scalar_tensor_tensor(out=qn.t[:, 0:tn], in0=qf.t[:, 0:tn], scalar=C.VT.t[:, gcol:gcol + 1],
                                                 in1=r_.t[:, 0:tn], op0=ALU.mult, op1=ALU.mult),
         reads=[qf, C.VT, r_], writes=[qn])
    return qn


def odd_proj_phase(C):
    P, I, S = C.P, C.I, C.S
    with contextlib.ExitStack() as ph:
        B = mixer_proj_common(C, ph, 1)
        R = mk_rot(C, ph)
        R["q_qf"] = Rot([P.sb(f"qqf{j}", [128, 512], F32, ph) for j in range(4)])
        R["q_sq"] = Rot([P.sb(f"qsq{j}", [128, 512], F32, ph) for j in range(2)])
        R["q_r"] = Rot([P.sb(f"qr{j}", [128, 512], F32, ph) for j in range(2)])
        R["q_qn"] = Rot([P.sb(f"qqn{j}", [128, 512], F32, ph) for j in range(2)])
        HF = Rot([P.sb(f"hf{j}", [128, 512], F32, ph) for j in range(3)])
        HB = Rot([P.sb(f"hb{j}", [128, 512], BF16, ph) for j in range(3)])
        QR = P.sb("QR", [128, 4, 512], F32, ph)
        KRrot = Rot([P.sb(f"krr{j}", [128, 512], F32, ph) for j in range(2)])
        O3rot = Rot([P.sb(f"o3r{j}", [128, 512], BF16, ph) for j in range(4)])
        pipe = Pipe()
        EX = P.sb("EX", [128, 8, 2, 512], F32, ph)
        dls = P.sb("dls", [128, 8], F32, ph)
        I1 = P.sb("I1", [128, 512], F32, ph)
        I2 = P.sb("I2", [128, 512], F32, ph)
        dlc = P.sb("dlc", [128, NCH], F32, ph)
        sm = C.small
        P.op("pool", lambda e: e.iota(I1.t[:].rearrange("p (c i) -> p c i", i=64), pattern=[[0, 8], [1, 64]], base=1,
                                      channel_multiplier=0, allow_small_or_imprecise_dtypes=True), writes=[I1])
        P.op("dve", lambda e: e.tensor_scalar(out=I2.t[:], in0=I1.t[:], scalar1=-1.0, scalar2=65.0, op0=ALU.mult, op1=ALU.add),
             reads=[I1], writes=[I2])
        for dirn in range(2):
            Ix, Iy = (I1, I2) if dirn == 0 else (I2, I1)
            for h in range(4):
                dh = dirn * 4 + h
                lg = sm.t[:, 8 + dh:9 + dh]
                nlg = sm.t[:, 16 + dh:17 + dh]
                P.op("act", lambda e, dh=dh, lg=lg, Ix=Ix: e.activation(out=EX.t[:, dh, 0, :], in_=Ix.t[:], func=AF.Exp, scale=lg),
                     reads=[Ix, sm], writes=[EX])
                P.op("act", lambda e, dh=dh, nlg=nlg, Ix=Ix: e.activation(out=EX.t[:, dh, 1, :], in_=Ix.t[:], func=AF.Exp, scale=nlg),
                     reads=[Ix, sm], writes=[EX])
                P.op("act", lambda e, dh=dh: e.activation(out=dls.t[:, dh:dh + 1], in_=sm.t[:, 8 + dh:9 + dh], func=AF.Exp, scale=64.0),
                     reads=[sm], writes=[dls])
                P.op("dve", lambda e: e.memset(dlc.t[:], 64.0), writes=[dlc])
                P.op("act", lambda e, lg=lg: e.activation(out=dlc.t[:], in_=dlc.t[:], func=AF.Exp, scale=lg), reads=[dlc, sm], writes=[dlc])
                P.dma("sp", S["GDL"][dh], dlc.t[:], dlc, reads=[dlc])
        xT, hT, Ct, St = B["xT"], B["hT"], B["Ct"], B["St"]
        XTv = S["XT"].rearrange("(c p) t -> p c t", p=128)
        wv = S["WOB"]
        state = {"w": 0, "p": 0}
        qs = 128.0 ** -0.5
        for (t0, tn, s) in token_tiles(True):
            P.dma("sp", xT.t[:, :, 0:tn], XTv[:, :, t0:t0 + tn], xT, writes=[xT])
            P.dma("sp", Ct.t[:, 0:tn], I["ropec"][:, t0:t0 + tn], Ct, writes=[Ct])
            P.dma("sp", St.t[:, 0:tn], I["ropes"][:, t0:t0 + tn], St, writes=[St])
            modnorm(C, (B["sq"], B["pss"], B["rstd"], B["tmp"]), xT, hT, tn, 1, 1, s)

            def item(j, pb, t0=t0, tn=tn):
                if j < 10:
                    isq = j < 8
                    qn = yield from qknorm(C, R, B, pb, tn, R_QG if isq else R_KG)
                    yield
                    ob = R["b16"].get()
                    rope_op(C, R, qn, tn, qs if isq else 1.0, ob.t[:, 0:tn], ob, Ct, St)
                    dst = S["BQ"] if isq else S["BK"]
                    r0 = j * 128 if isq else (j - 8) * 128
                    P.dma("sp", dst[r0:r0 + 128, t0:t0 + tn], ob.t[:, 0:tn], ob, reads=[ob])
                elif j < 12 or (20 <= j < 28):
                    vb = HB.get()
                    P.op("act", lambda e: e.copy(out=vb.t[:, 0:tn], in_=pb.t[:, 0:tn]), reads=[pb], writes=[vb])
                    if j < 12:
                        dst = S["BV"][t0:t0 + tn, (j - 10) * 128:(j - 9) * 128]
                    else:
                        dst = S["GV"][t0:t0 + tn, (j - 20) * 128:(j - 19) * 128]
                    yield
                    store_tokmajor(C, R, vb, tn, dst)
                elif j < 16:
                    h = j - 12
                    qf = HF.get()
                    P.op("act", lambda e: e.copy(out=qf.t[:, 0:tn], in_=pb.t[:, 0:tn]), reads=[pb], writes=[qf])
                    yield
                    rope_op(C, R, qf, tn, 1.0, QR.t[:, h, 0:tn], QR, Ct, St)
                elif j < 20:
                    h = j - 16
                    kf = HF.get()
                    P.op("act", lambda e: e.copy(out=kf.t[:, 0:tn], in_=pb.t[:, 0:tn]), reads=[pb], writes=[kf])
                    yield
                    kr = KRrot.get()
                    rope_op(C, R, kf, tn, qs, kr.t[:, 0:tn], kr, Ct, St)
                    o3s = []
                    for dirn in range(2):
                        dh = dirn * 4 + h
                        o1 = R["b16"].get()
                        P.op("dve", lambda e, dh=dh, o1=o1: e.tensor_tensor(out=o1.t[:, 0:tn], in0=QR.t[:, h, 0:tn],
                                                                            in1=EX.t[:, dh, 0, 0:tn], op=ALU.mult),
                             reads=[QR, EX], writes=[o1])
                        P.dma("sp", S["GQ"][dh][:, t0:t0 + tn], o1.t[:, 0:tn], o1, reads=[o1])
                        o2 = R["b16"].get()
                        P.op("dve", lambda e, dh=dh, o2=o2: e.tensor_tensor(out=o2.t[:, 0:tn], in0=kr.t[:, 0:tn],
                                                                            in1=EX.t[:, dh, 1, 0:tn], op=ALU.mult),
                             reads=[kr, EX], writes=[o2])
                        P.dma("sp", S["GK"][dh][:, t0:t0 + tn], o2.t[:, 0:tn], o2, reads=[o2])
                        o3 = O3rot.get()
                        P.op("dve", lambda e, dh=dh, o3=o3: e.scalar_tensor_tensor(
                            out=o3.t[:, 0:tn], in0=kr.t[:, 0:tn], scalar=dls.t[:, dh:dh + 1], in1=EX.t[:, dh, 1, 0:tn],
                            op0=ALU.mult, op1=ALU.mult), reads=[kr, EX, dls], writes=[o3])
                        o3s.append((dh, o3))
                    yield
                    for dh, o3 in o3s:
                        store_tokmajor(C, R, o3, tn, S["GKH"][dh][t0:t0 + tn, :])
                else:
                    gt = R["f32"].get()
                    P.op("act", lambda e: e.activation(out=gt.t[:, 0:tn], in_=pb.t[:, 0:tn], func=AF.Silu), reads=[pb], writes=[gt])
                    P.dma("sp", S["GG"][(j - 28) * 128:(j - 27) * 128, t0:t0 + tn], gt.t[:, 0:tn], gt, reads=[gt])

            def cb(j, pb):
                pipe.advance()
                pipe.start(item(j, pb))

            proj_chunks(C, ph, hT, tn, wv, list(range(36)), B["wb"], B["psp"], cb, state)
            pipe.drain()
        P.end_phase()


def odd_mixer(C):
    P, S = C.P, C.S
    odd_proj_phase(C)
    if C.stop == "m1p":
        return
    gla_scan_phase(C, 4, 256, 0, lambda d: d)
    gla_scan_phase(C, 4, 256, 1, lambda d: d)
    headnorm_phase(C, 4, 2, R_RG, 1024, token_tiles(False))
    lat_q = [(NCTX + 512 * i, 512, T // 128) for i in range(8)]
    units = []
    for kv in range(2):
        units.append(dict(ks=[S["BK"][kv * 128:(kv + 1) * 128, :]],
                          maps=[(S["BQ"][(kv * 4 + m) * 128:(kv * 4 + m + 1) * 128, :], 0) for m in range(4)],
                          v=S["BV"][:, kv * 128:(kv + 1) * 128], qtiles=lat_q, kv=kv))

    def alloc(ph):
        return {"yb": Rot([P.sb(f"gyb{j}", [128, 512], BF16, ph) for j in range(4)])}

    def finish(u, t0, tn, obs, X):
        for m in range(4):
            yb = X["yb"].get()
            P.op("act", lambda e, m=m, yb=yb: e.copy(out=yb.t[:, 0:tn], in_=obs[m].t[:, 0, 0:tn]), reads=[obs[m]], writes=[yb])
            r0 = (u["kv"] * 4 + m) * 128
            P.dma("sp", S["YM"][r0:r0 + 128, t0:t0 + tn], yb.t[:, 0:tn], yb, reads=[yb])

    attention_phase(C, units, 128, finish, alloc)
    if C.stop == "m1a":
        return
    out_proj_phase(C, 1, C.I["odd_w_out"], token_tiles(False))


def host_consts():
    import ml_dtypes
    rows = SEQ // 64
    row = np.repeat(np.arange(rows), 64)
    col = np.tile(np.arange(64), rows)
    inv = (10000.0 ** (-np.arange(32, dtype=np.float32) / 32)).astype(np.float32)
    pos = np.stack([row, col], 0).astype(np.float32)
    ang = pos[:, None, :] * inv[None, :, None]
    cosv = np.cos(ang).astype(np.float32)
    sinv = np.sin(ang).astype(np.float32)
    ropec = np.ones((128, T), np.float32)
    ropes = np.zeros((128, T), np.float32)
    for ax in range(2):
        for half in range(2):
            p0 = ax * 64 + half * 32
            ropec[p0:p0 + 32, NCTX:] = cosv[ax]
            ropes[p0:p0 + 32, NCTX:] = sinv[ax]
    rmat = np.zeros((128, 128), np.float32)
    for ax in range(2):
        for f in range(32):
            p1 = ax * 64 + f
            p2 = ax * 64 + 32 + f
            rmat[p2, p1] = -1.0
            rmat[p1, p2] = 1.0
    jj, ii = np.meshgrid(np.arange(64), np.arange(64), indexing="ij")
    masks = np.concatenate([(jj <= ii), (jj >= ii)], axis=1).astype(np.float32)
    return ropec, ropes, rmat, masks


_NC_CACHE = {}


def make_in_maps(inp):
    ropec, ropes, rmat, masks = host_consts()
    f = lambda a: np.ascontiguousarray(np.asarray(a, dtype=np.float32))
    shared = {
        "mod_w": f(inp["mod_w"]), "ffn_w13": f(inp["ffn_w13"]), "ffn_w2": f(inp["ffn_w2"]),
        "even_w_in": f(inp["even_w_in"][0]), "even_w_out": f(inp["even_w_out"][0]),
        "odd_w_in": f(inp["odd_w_in"][0]), "odd_w_out": f(inp["odd_w_out"][0]),
        "ropec": ropec, "ropes": ropes, "rmat": rmat, "masks": masks,
        "lamrep": f(np.broadcast_to(np.asarray(inp["diff_lambda"][0]).reshape(1, 512), (128, 512))),
        "decrep": f(np.broadcast_to(np.asarray(inp["ret_decay_logit"][0]).reshape(1, 8), (128, 8))),
    }
    maps = []
    for b in range(8):
        vecs = np.zeros((R_TOT, 128), np.float32)
        vecs[R_C:R_C + 16] = np.asarray(inp["c"][b]).reshape(16, 128)
        vecs[R_CCTX:R_CCTX + 16] = np.asarray(inp["c_ctx"]).reshape(16, 128)
        vecs[R_NG:R_NG + 96] = np.asarray(inp["norm_g"]).reshape(96, 128)
        vecs[R_FG:R_FG + 16] = np.asarray(inp["final_norm_g"]).reshape(16, 128)
        vecs[R_LB:R_LB + 24] = np.asarray(inp["hgrn_lb_logits"]).reshape(24, 128)
        vecs[R_HGG] = np.asarray(inp["hgrn_norm_g"][0])
        vecs[R_DAG:R_DAG + 2] = np.asarray(inp["diff_norm_g"][0]).reshape(2, 128)
        vecs[R_QG] = np.asarray(inp["gqa_q_norm_g"][0])
        vecs[R_KG] = np.asarray(inp["gqa_k_norm_g"][0])
        vecs[R_RG:R_RG + 2] = np.asarray(inp["ret_norm_g"][0]).reshape(2, 128)
        vecs[R_MB:R_MB + 288] = np.asarray(inp["mod_b"]).reshape(288, 128)
        m = dict(shared)
        m["x"] = f(inp["x"][b])
        m["ctx"] = f(inp["ctx"][b])
        m["vecs"] = vecs
        maps.append(m)
    return maps


def kernel(**inp):
    if "nc" not in _NC_CACHE:
        _NC_CACHE["nc"] = build_program()
    nc = _NC_CACHE["nc"]
    maps = make_in_maps(inp)
    res = run_bass_kernel_spmd(nc, maps, core_ids=list(range(8)))
    return np.stack([np.asarray(r["out"], dtype=np.float32) for r in res.results], axis=0)
```
